# Optimizing a Trainium2 kernel written in Bass

```python
import math
import jax, jax.numpy as jnp
from jax import lax
import numpy as np

D_MODEL = 2048
BATCH = 2
SEQ = 4096
DEPTH = 4
DEC_BATCH = 32
DEC_SEQ = 8
PAST_LEN = 16384
PAGE_SIZE = 128

HEAD_DIM = 64
N_HEADS = D_MODEL // HEAD_DIM
N_KV_HEADS = N_HEADS // 8
GQA_GROUP = N_HEADS // N_KV_HEADS
WINDOW = 128
ATTN_BLOCK = WINDOW
ROPE_THETA = 10000.0
SSM_WIDTH = D_MODEL
SSM_GROUP = 16
N_SSM_GROUPS = SSM_WIDTH // SSM_GROUP
SSM_STATE = 64
LOG_STEP_MIN = math.log(1e-3)
LOG_STEP_MAX = math.log(1e-1)
N_MEM = 256
N_XHEADS = 4
XHEAD_DIM = 128
D_FF = 5632
CONV_W = 3
N_ATTN_LAYERS = (DEPTH + 1) // 2
N_SSM_LAYERS = DEPTH // 2
NORM_EPS = 1e-6
NEG_INF = -1e30

kernel_name = 'swa_sink_s5_memxattn_convffn_step'


def rmsnorm(x, g):
    x32 = x.astype(jnp.float32)
    y = x32 * lax.rsqrt(jnp.mean(x32 * x32, axis=-1, keepdims=True) + NORM_EPS)
    return (y * g.astype(jnp.float32)).astype(x.dtype)


def rope(x, pos):
    half = HEAD_DIM // 2
    inv_freq = ROPE_THETA ** (-jnp.arange(half, dtype=jnp.float32) * 2.0 / HEAD_DIM)
    ang = pos.astype(jnp.float32)[:, None] * inv_freq[None, :]
    cos = jnp.cos(ang)[:, None, :]
    sin = jnp.sin(ang)[:, None, :]
    x1 = x[..., :half].astype(jnp.float32)
    x2 = x[..., half:].astype(jnp.float32)
    return jnp.concatenate([x1 * cos - x2 * sin, x2 * cos + x1 * sin], axis=-1).astype(x.dtype)


def split_qkv(h, w_qkv):
    B, L, _ = h.shape
    qkv = h @ w_qkv
    nq = N_HEADS * HEAD_DIM
    nk = N_KV_HEADS * HEAD_DIM
    q = qkv[..., :nq].reshape(B, L, N_HEADS, HEAD_DIM)
    k = qkv[..., nq:nq + nk].reshape(B, L, N_KV_HEADS, HEAD_DIM)
    v = qkv[..., nq + nk:].reshape(B, L, N_KV_HEADS, HEAD_DIM)
    return q, k, v


def sink_attention(q, k, v, mask, sinks):
    s = jnp.einsum('...qkgd,...skd->...kgqs', q.astype(jnp.float32), k.astype(jnp.float32)) * (HEAD_DIM ** -0.5)
    s = jnp.where(mask, s, NEG_INF)
    sk = jnp.broadcast_to(sinks.astype(jnp.float32).reshape(N_KV_HEADS, GQA_GROUP, 1, 1), s.shape[:-1] + (1,))
    p = jax.nn.softmax(jnp.concatenate([s, sk], axis=-1), axis=-1)[..., :-1]
    return jnp.einsum('...kgqs,...skd->...qkgd', p, v.astype(jnp.float32))


def swa_prompt(h, w_qkv, w_o, sinks):
    B, L, _ = h.shape
    q, k, v = split_qkv(h, w_qkv)
    pos = jnp.arange(L)
    q = rope(q, pos)
    k = rope(k, pos)
    nb = L // ATTN_BLOCK
    qb = q.reshape(B, nb, ATTN_BLOCK, N_KV_HEADS, GQA_GROUP, HEAD_DIM)
    kb = k.reshape(B, nb, ATTN_BLOCK, N_KV_HEADS, HEAD_DIM)
    vb = v.reshape(B, nb, ATTN_BLOCK, N_KV_HEADS, HEAD_DIM)

    def with_prev(t):
        prev = jnp.pad(t[:, :-1], ((0, 0), (1, 0), (0, 0), (0, 0), (0, 0)))
        return jnp.concatenate([prev, t], axis=2)

    kk, vv = with_prev(kb), with_prev(vb)
    qi = jnp.arange(ATTN_BLOCK)[:, None]
    si = jnp.arange(2 * ATTN_BLOCK)[None, :]
    d = ATTN_BLOCK + qi - si
    kpos = (jnp.arange(nb)[:, None, None] - 1) * ATTN_BLOCK + si[None]
    mask = (d >= 0)[None] & (d < WINDOW)[None] & (kpos >= 0)
    o = sink_attention(qb, kk, vv, mask[None, :, None, None], sinks)
    y = o.reshape(B, L, N_HEADS * HEAD_DIM).astype(h.dtype) @ w_o
    wb = min(WINDOW, L)
    return y, k[:, L - wb:], v[:, L - wb:]


def swa_sample(h, ck, cv, w_qkv, w_o, sinks):
    B, L, _ = h.shape
    wb = ck.shape[1]
    q, k, v = split_qkv(h, w_qkv)
    pos = PAST_LEN + jnp.arange(L)
    q = rope(q, pos)
    k = rope(k, pos)
    kk = jnp.concatenate([ck.astype(k.dtype), k], axis=1)
    vv = jnp.concatenate([cv.astype(v.dtype), v], axis=1)
    kpos = jnp.concatenate([PAST_LEN - wb + jnp.arange(wb), pos])
    d = pos[:, None] - kpos[None, :]
    mask = (d >= 0) & (d < WINDOW)
    o = sink_attention(q.reshape(B, L, N_KV_HEADS, GQA_GROUP, HEAD_DIM), kk, vv, mask, sinks)
    y = o.reshape(B, L, N_HEADS * HEAD_DIM).astype(h.dtype) @ w_o
    return y, kk[:, -wb:], vv[:, -wb:]


def _lin_combine(e1, e2):
    a1, b1 = e1
    a2, b2 = e2
    return a1 * a2, a2 * b1 + b2


def s5_mix(h, h0_re, h0_im, w_in, lam_re, lam_im, log_step, b_re, b_im, c_re, c_im, d_skip, w_glu):
    B, L, _ = h.shape
    f32 = jnp.float32
    u = (h @ w_in).astype(f32).reshape(B, L, N_SSM_GROUPS, SSM_GROUP)
    lam = lax.complex(lam_re.astype(f32), lam_im.astype(f32))
    dt = jnp.exp(log_step.astype(f32))[:, None]
    abar = jnp.exp(lam * dt)
    bbar = ((abar - 1.0) / lam)[..., None] * lax.complex(b_re.astype(f32), b_im.astype(f32))
    bu = jnp.einsum('gph,blgh->blgp', bbar, u.astype(jnp.complex64))
    h0 = lax.complex(h0_re.astype(f32), h0_im.astype(f32))
    bu = bu.at[:, 0].add(abar * h0)
    a = jnp.broadcast_to(abar, (1, L) + abar.shape)
    _, hs = lax.associative_scan(_lin_combine, (a, bu), axis=1)
    cc = lax.complex(c_re.astype(f32), c_im.astype(f32))
    y = jnp.einsum('ghp,blgp->blgh', cc, hs).real + d_skip.astype(f32) * u
    z = jax.nn.gelu(y.reshape(B, L, SSM_WIDTH)).astype(h.dtype)
    val, gate = jnp.split(z @ w_glu, 2, axis=-1)
    out = val * jax.nn.sigmoid(gate)
    h_last = hs[:, -1]
    return out, h_last.real, h_last.imag


def memory_kv(mem, g_mem, w_k, w_v):
    B, M, _ = mem.shape
    mn = rmsnorm(mem, g_mem)
    k = (mn @ w_k).reshape(B, M, N_XHEADS, XHEAD_DIM)
    v = (mn @ w_v).reshape(B, M, N_XHEADS, XHEAD_DIM)
    return k, v


def cross_attend(h, mk, mv, w_q, w_o):
    B, L, _ = h.shape
    q = (h @ w_q).reshape(B, L, N_XHEADS, XHEAD_DIM).astype(jnp.float32)
    s = jnp.einsum('blhd,bmhd->bhlm', q, mk.astype(jnp.float32)) * (XHEAD_DIM ** -0.5)
    p = jax.nn.softmax(s, axis=-1)
    o = jnp.einsum('bhlm,bmhd->blhd', p, mv.astype(jnp.float32))
    return o.reshape(B, L, N_XHEADS * XHEAD_DIM).astype(h.dtype) @ w_o


def conv_ffn(h, conv_state, w_up, conv_w, conv_b, w_down):
    L = h.shape[1]
    up = h @ w_up
    ext = jnp.concatenate([conv_state.astype(up.dtype), up], axis=1)
    c = conv_b + sum(conv_w[j] * ext[:, j:j + L] for j in range(CONV_W))
    a, v = jnp.split(c, 2, axis=-1)
    return (jax.nn.silu(a) * v) @ w_down, ext[:, L:]


def setup_inputs(seed: int = 0) -> dict:
    key = jax.random.key(seed)
    ks = iter(jax.random.split(key, 48))
    f32 = jnp.float32

    def nrm(shape, scale):
        return jax.random.normal(next(ks), shape, f32) * scale

    def gain(shape):
        return 1.0 + nrm(shape, 0.05)

    wb = min(WINDOW, PAST_LEN)
    qkv_cols = (N_HEADS + 2 * N_KV_HEADS) * HEAD_DIM
    xw = N_XHEADS * XHEAD_DIM
    lam_im = jnp.pi * jnp.arange(SSM_STATE, dtype=f32)[None, None, :] + nrm((N_SSM_LAYERS, N_SSM_GROUPS, SSM_STATE), 0.01)
    return {
        'x_prompt': nrm((BATCH, SEQ, D_MODEL), 1.0),
        'x_sample': nrm((DEC_BATCH, DEC_SEQ, D_MODEL), 1.0),
        'mem_prompt': nrm((BATCH, N_MEM, D_MODEL), 1.0),
        'cache_swa_k': nrm((N_ATTN_LAYERS, DEC_BATCH, wb, N_KV_HEADS, HEAD_DIM), 1.0),
        'cache_swa_v': nrm((N_ATTN_LAYERS, DEC_BATCH, wb, N_KV_HEADS, HEAD_DIM), 1.0),
        'state_ssm_re': nrm((N_SSM_LAYERS, DEC_BATCH, N_SSM_GROUPS, SSM_STATE), 0.1),
        'state_ssm_im': nrm((N_SSM_LAYERS, DEC_BATCH, N_SSM_GROUPS, SSM_STATE), 0.1),
        'state_ffn_conv': nrm((DEPTH, DEC_BATCH, CONV_W - 1, 2 * D_FF), 1.0),
        'cache_mem_k': nrm((DEPTH, DEC_BATCH, N_MEM, N_XHEADS, XHEAD_DIM), 1.0),
        'cache_mem_v': nrm((DEPTH, DEC_BATCH, N_MEM, N_XHEADS, XHEAD_DIM), 1.0),
        'g_mix_pre': gain((DEPTH, D_MODEL)),
        'g_mix_post': gain((DEPTH, D_MODEL)),
        'w_qkv': nrm((N_ATTN_LAYERS, D_MODEL, qkv_cols), D_MODEL ** -0.5),
        'w_attn_o': nrm((N_ATTN_LAYERS, N_HEADS * HEAD_DIM, D_MODEL), (N_HEADS * HEAD_DIM) ** -0.5),
        'attn_sinks': nrm((N_ATTN_LAYERS, N_HEADS), 1.0),
        'w_ssm_in': nrm((N_SSM_LAYERS, D_MODEL, SSM_WIDTH), D_MODEL ** -0.5),
        'ssm_lambda_re': -0.5 + nrm((N_SSM_LAYERS, N_SSM_GROUPS, SSM_STATE), 0.01),
        'ssm_lambda_im': lam_im,
        'ssm_log_step': jax.random.uniform(next(ks), (N_SSM_LAYERS, N_SSM_GROUPS), f32, LOG_STEP_MIN, LOG_STEP_MAX),
        'ssm_b_re': nrm((N_SSM_LAYERS, N_SSM_GROUPS, SSM_STATE, SSM_GROUP), (2 * SSM_GROUP) ** -0.5),
        'ssm_b_im': nrm((N_SSM_LAYERS, N_SSM_GROUPS, SSM_STATE, SSM_GROUP), (2 * SSM_GROUP) ** -0.5),
        'ssm_c_re': nrm((N_SSM_LAYERS, N_SSM_GROUPS, SSM_GROUP, SSM_STATE), SSM_STATE ** -0.5),
        'ssm_c_im': nrm((N_SSM_LAYERS, N_SSM_GROUPS, SSM_GROUP, SSM_STATE), SSM_STATE ** -0.5),
        'ssm_d': nrm((N_SSM_LAYERS, N_SSM_GROUPS, SSM_GROUP), 1.0),
        'w_ssm_glu': nrm((N_SSM_LAYERS, SSM_WIDTH, 2 * D_MODEL), SSM_WIDTH ** -0.5),
        'g_x_pre': gain((DEPTH, D_MODEL)),
        'g_x_post': gain((DEPTH, D_MODEL)),
        'g_mem': gain((DEPTH, D_MODEL)),
        'w_x_q': nrm((DEPTH, D_MODEL, xw), D_MODEL ** -0.5),
        'w_mem_k': nrm((DEPTH, D_MODEL, xw), D_MODEL ** -0.5),
        'w_mem_v': nrm((DEPTH, D_MODEL, xw), D_MODEL ** -0.5),
        'w_x_o': nrm((DEPTH, xw, D_MODEL), xw ** -0.5),
        'g_ffn_pre': gain((DEPTH, D_MODEL)),
        'g_ffn_post': gain((DEPTH, D_MODEL)),
        'w_ffn_up': nrm((DEPTH, D_MODEL, 2 * D_FF), D_MODEL ** -0.5),
        'ffn_conv_w': nrm((DEPTH, CONV_W, 2 * D_FF), CONV_W ** -0.5),
        'ffn_conv_b': nrm((DEPTH, 2 * D_FF), 0.02),
        'w_ffn_down': nrm((DEPTH, D_FF, D_MODEL), D_FF ** -0.5),
    }


def reference(x_prompt, x_sample, mem_prompt, cache_swa_k, cache_swa_v, state_ssm_re, state_ssm_im,
              state_ffn_conv, cache_mem_k, cache_mem_v, g_mix_pre, g_mix_post, w_qkv, w_attn_o, attn_sinks,
              w_ssm_in, ssm_lambda_re, ssm_lambda_im, ssm_log_step, ssm_b_re, ssm_b_im, ssm_c_re, ssm_c_im,
              ssm_d, w_ssm_glu, g_x_pre, g_x_post, g_mem, w_x_q, w_mem_k, w_mem_v, w_x_o,
              g_ffn_pre, g_ffn_post, w_ffn_up, ffn_conv_w, ffn_conv_b, w_ffn_down):
    xp, xs = x_prompt, x_sample
    swa_kp, swa_vp, swa_ks, swa_vs = [], [], [], []
    ssm_rp, ssm_ip, ssm_rs, ssm_is = [], [], [], []
    conv_p, conv_s, memk_p, memv_p = [], [], [], []
    for i in range(DEPTH):
        j = i // 2
        hp = rmsnorm(xp, g_mix_pre[i])
        hs = rmsnorm(xs, g_mix_pre[i])
        if i % 2 == 0:
            op, kp, vp = swa_prompt(hp, w_qkv[j], w_attn_o[j], attn_sinks[j])
            os_, kn, vn = swa_sample(hs, cache_swa_k[j], cache_swa_v[j], w_qkv[j], w_attn_o[j], attn_sinks[j])
            swa_kp.append(kp); swa_vp.append(vp); swa_ks.append(kn); swa_vs.append(vn)
        else:
            ssm_w = (w_ssm_in[j], ssm_lambda_re[j], ssm_lambda_im[j], ssm_log_step[j], ssm_b_re[j], ssm_b_im[j],
                     ssm_c_re[j], ssm_c_im[j], ssm_d[j], w_ssm_glu[j])
            zero_state = jnp.zeros((xp.shape[0], N_SSM_GROUPS, SSM_STATE), jnp.float32)
            op, rp, ip = s5_mix(hp, zero_state, zero_state, *ssm_w)
            os_, rn, im_ = s5_mix(hs, state_ssm_re[j], state_ssm_im[j], *ssm_w)
            ssm_rp.append(rp); ssm_ip.append(ip); ssm_rs.append(rn); ssm_is.append(im_)
        xp = xp + rmsnorm(op, g_mix_post[i])
        xs = xs + rmsnorm(os_, g_mix_post[i])
        mk, mv = memory_kv(mem_prompt, g_mem[i], w_mem_k[i], w_mem_v[i])
        memk_p.append(mk); memv_p.append(mv)
        xp = xp + rmsnorm(cross_attend(rmsnorm(xp, g_x_pre[i]), mk, mv, w_x_q[i], w_x_o[i]), g_x_post[i])
        xs = xs + rmsnorm(cross_attend(rmsnorm(xs, g_x_pre[i]), cache_mem_k[i], cache_mem_v[i], w_x_q[i], w_x_o[i]), g_x_post[i])
        zero_conv = jnp.zeros((xp.shape[0], CONV_W - 1, 2 * D_FF), xp.dtype)
        fp, cp = conv_ffn(rmsnorm(xp, g_ffn_pre[i]), zero_conv, w_ffn_up[i], ffn_conv_w[i], ffn_conv_b[i], w_ffn_down[i])
        fs, cs = conv_ffn(rmsnorm(xs, g_ffn_pre[i]), state_ffn_conv[i], w_ffn_up[i], ffn_conv_w[i], ffn_conv_b[i], w_ffn_down[i])
        conv_p.append(cp); conv_s.append(cs)
        xp = xp + rmsnorm(fp, g_ffn_post[i])
        xs = xs + rmsnorm(fs, g_ffn_post[i])
    return (xp, xs,
            jnp.stack(swa_kp), jnp.stack(swa_vp), jnp.stack(swa_ks), jnp.stack(swa_vs),
            jnp.stack(ssm_rp), jnp.stack(ssm_ip), jnp.stack(ssm_rs), jnp.stack(ssm_is),
            jnp.stack(conv_p), jnp.stack(conv_s), jnp.stack(memk_p), jnp.stack(memv_p))
```

```python
import math
import numpy as np
from contextlib import ExitStack
import concourse.bass as bass
import concourse.mybir as mybir
from concourse.bass_utils import run_bass_kernel_spmd

F32 = mybir.dt.float32
BF16 = mybir.dt.bfloat16
AF = mybir.ActivationFunctionType
ALU = mybir.AluOpType
AX = mybir.AxisListType

NCORES = 8
D = 2048
KC = 16
PT = 1024
ST = 32
NT = PT + ST
NTH = NT + 2
DEPTH = 4
DFF = 5632
FC = DFF // 128
HD = 64
NH = 32
NKV = 4
NMEM = 256
XH = 4
XD = 128
G = 128
PS = 64
HG = 16
EPS = 1e-6
TILES = [(0, 512), (512, 512), (1024, 32)]
NEG = -30000.0
PAST = 16384


class Buf:
    __slots__ = ("name", "w", "r", "excl")

    def __init__(self, name, inherit=None, excl=False):
        self.name = name
        self.excl = excl
        self.w = {}
        self.r = dict(inherit) if inherit else {}

    def hazards(self):
        d = dict(self.w)
        for k, v in self.r.items():
            if k not in d or d[k][1] < v[1]:
                d[k] = v
        return d


class DSem:
    __slots__ = ("sem", "cnt", "shared")

    def __init__(self, sem):
        self.sem = sem
        self.cnt = 0
        self.shared = False


class Eng:
    def __init__(self, name, h, sem):
        self.name = name
        self.h = h
        self.sem = sem
        self.cnt = 0
        self.waited = {}
        self.pend = False
        self.n = 0


def _merge(d, src):
    for k, v in src.items():
        if k not in d or d[k][1] < v[1]:
            d[k] = v


class Prog:
    def __init__(self, nc, es):
        self.nc = nc
        self.es = es
        self.nsem = 0
        self.pe = Eng("pe", nc.tensor, self.sem("pe"))
        self.act = Eng("act", nc.scalar, self.sem("act"))
        self.dve = Eng("dve", nc.vector, self.sem("dve"))
        self.pool = Eng("pool", nc.gpsimd, self.sem("pool"))
        self.sp = Eng("sp", nc.sync, self.sem("sp"))
        self.engs = [self.pe, self.act, self.dve, self.pool, self.sp]
        self.all_dsems = []

    def sem(self, name):
        self.nsem += 1
        return self.es.enter_context(self.nc.semaphore(name))

    NSHARED = 6

    def dsem(self, name, dedicated=False):
        if dedicated:
            d = DSem(self.sem(name))
            self.all_dsems.append(d)
            return d
        if not hasattr(self, "_shared"):
            self._shared = []
            self._sh_i = 0
        if len(self._shared) < self.NSHARED:
            d = DSem(self.sem(f"sh{len(self._shared)}"))
            d.shared = True
            self._shared.append(d)
            self.all_dsems.append(d)
            return d
        d = self._shared[self._sh_i % self.NSHARED]
        self._sh_i += 1
        return d

    def sb(self, name, shape, dt):
        return self.es.enter_context(self.nc.sbuf_tensor("s_" + name, shape, dt))

    def _waits(self, E, r, w):
        deps = {}
        for b in r:
            _merge(deps, b.w)
            if b.excl:
                _merge(deps, b.r)
        for b in w:
            _merge(deps, b.w)
            _merge(deps, b.r)
        for k, (sem, val) in deps.items():
            if sem is E.sem and E.name == "pe":
                continue
            if E.waited.get(k, 0) >= val:
                continue
            E.h.wait_ge(sem, val)
            E.waited[k] = val

    def op(self, E, fn, r=(), w=(), sig=True):
        self._waits(E, r, w)
        ins = fn(E.h)
        E.n += 1
        tok = (E.sem, E.cnt + 1)
        k = id(E.sem)
        for b in r:
            b.r[k] = tok
        for b in w:
            b.w = {k: tok}
            b.r = {}
        if sig:
            ins.then_inc(E.sem, 1)
            E.cnt += 1
            E.pend = False
        else:
            E.pend = True
        return ins

    def dma(self, Q, out, in_, own, r=(), w=(), **kw):
        self._waits(Q, r, w)
        if own.shared and own.cnt > 0 and Q.waited.get(id(own.sem), 0) < own.cnt:
            Q.h.wait_ge(own.sem, own.cnt)
            Q.waited[id(own.sem)] = own.cnt
        ins = Q.h.dma_start(out=out, in_=in_, **kw)
        own.cnt += 16
        ins.then_inc(own.sem, 16)
        tok = (own.sem, own.cnt)
        k = id(own.sem)
        for b in r:
            b.r[k] = tok
        for b in w:
            b.w = {k: tok}
            b.r = {}
        return ins

    def wait_tok(self, E, tokdict):
        for k, (sem, val) in tokdict.items():
            if sem is E.sem or E.waited.get(k, 0) >= val:
                continue
            E.h.wait_ge(sem, val)
            E.waited[k] = val


class Arena:
    def __init__(self, name):
        self.name = name
        self.live = []
        self.inh = {}

    def reset(self):
        self.inh = dict(self.inh)
        for b in self.live:
            _merge(self.inh, b.hazards())
        self.live = []

    def buf(self, name):
        b = Buf(self.name + "." + name, self.inh)
        self.live.append(b)
        return b

    def bufs(self, name, n):
        return [self.buf(f"{name}{i}") for i in range(n)]


class Kernel:
    def __init__(self, cfg):
        self.cfg = cfg
        self.nc = bass.Bass("TRN2", target_bir_lowering=False)
        self.din = {}
        self.dout = {}

    def inp(self, name, shape, dt=F32):
        t = self.nc.dram_tensor(name, list(shape), dt, kind="ExternalInput")
        self.din[name] = t
        return t

    def outp(self, name, shape, dt=F32):
        t = self.nc.dram_tensor(name, list(shape), dt, kind="ExternalOutput")
        self.dout[name] = t
        return t

    def build(self):
        with ExitStack() as es:
            self.es = es
            self.p = Prog(self.nc, es)
            self._build()
        return self.nc

    def psum_init(self):
        p = self.p
        self.ps_t = [es_enter(self.es, self.nc.psum_tensor(f"ps{i}", [128, 512], F32)) for i in range(8)]
        self.ps_b = [Buf(f"ps{i}", excl=True) for i in range(8)]
        self.ps_i = 0

    def psum(self):
        i = self.ps_i
        self.ps_i = (i + 1) % 8
        return self.ps_t[i], self.ps_b[i]

    def ws_init(self):
        p = self.p
        self.NSLOT = 2
        self.ws_t = [p.sb(f"wslot{i}", [128, 4096], BF16) for i in range(self.NSLOT)]
        self.ws_b = [Buf(f"wslot{i}") for i in range(self.NSLOT)]
        self.ws_s = [p.dsem(f"wsem{i}", dedicated=True) for i in range(self.NSLOT)]
        self.ws_jobs = []
        self.ws_issued = 0
        self.ws_next = 0

    def ws_issue_upto(self, n):
        p = self.p
        while self.ws_issued < min(n, len(self.ws_jobs)):
            j = self.ws_issued
            s = j % self.NSLOT
            tag, fn = self.ws_jobs[j]

            def issue(out, in_, s=s):
                p.dma(p.pool, out, in_, self.ws_s[s], w=[self.ws_b[s]])
            fn(self.ws_t[s], issue)
            self.ws_issued += 1

    def ws_get(self, tag):
        j = self.ws_next
        assert self.ws_jobs[j][0] == tag, (self.ws_jobs[j][0], tag)
        self.ws_issue_upto(j + 1)
        self.ws_next += 1
        s = j % self.NSLOT
        return self.ws_t[s], self.ws_b[s]

    def ws_prefetch(self):
        self.ws_issue_upto(self.ws_next + self.NSLOT)


def es_enter(es, cm):
    return es.enter_context(cm)


def es_enter(es, cm):
    return es.enter_context(cm)


class Bump:
    def __init__(self, k):
        self.k = k
        self.off = 0

    def take(self, dt, *dims):
        n = int(np.prod(dims)) * (4 if dt == F32 else 2)
        ap = self.k.wv(self.off, dt, *dims)
        self.off += (n + 3) // 4 * 4
        return ap


def _wv(self, off, dt, *dims):
    n = int(np.prod(dims))
    esz = 4 if dt == F32 else 2
    assert off % 4 == 0
    nw = (n * esz + 3) // 4
    assert off // 4 + nw <= self.WN, ("arena overflow", off, n * esz, self.WN * 4)
    ap = self.W[:, off // 4: off // 4 + nw]
    if dt != F32:
        ap = ap.bitcast(dt)
        if n != nw * 2:
            ap = ap[:, 0:n]
    if len(dims) == 2:
        ap = ap.rearrange("p (a b) -> p a b", b=dims[1])
    elif len(dims) == 3:
        ap = ap.rearrange("p (a b c) -> p a b c", b=dims[1], c=dims[2])
    return ap


def _setup(self):
    nc, p, cfg = self.nc, self.p, self.cfg
    nL = len(cfg["layers"])
    self.psum_init()
    self.ws_init()
    self.X = p.sb("X", [128, KC, NTH], F32)
    self.xb = [[Buf(f"x{c}_{t}") for t in range(4)] for c in range(KC)]
    self.WN = cfg.get("WN", 29700)
    self.W = p.sb("W", [128, self.WN], F32)
    self.arena = Arena("W")
    self.gains = p.sb("gains", [128, DEPTH * 7 * KC], F32)
    self.gains_b = Buf("gains")
    self.convw_b = Buf("convw")
    self.cstate_b = Buf("cstate")
    self.carryA_b, self.carryB_b, self.outc_p_b, self.outc_s_b = Buf("cA"), Buf("cB"), Buf("ocp"), Buf("ocs")
    self.ks32 = p.sb("ks32", [32, 256], F32)
    self.ks32_b = Buf("ks32")
    self.cb16 = p.sb("cb16", [128, 4, 128], BF16)
    self.cb16_b = Buf("cb16")
    self.ld_x = p.dsem("ld_x")
    self.ld_g = p.dsem("ld_g")
    self.ld_c = p.dsem("ld_c")
    self.ld_cw = p.dsem("ld_cw")
    self.ld_cs = p.dsem("ld_cs")
    self.st = p.dsem("st_out")
    self.st_cp = p.dsem("st_cp")
    self.st_cs = p.dsem("st_cs")
    self.d_xT = self.inp("xT", [D, NT]).ap()
    self.d_gains = self.inp("gains", [128, DEPTH * 7 * KC]).ap()
    self.d_convw = self.inp("convw", [DEPTH, 128, 88 * 4]).ap()
    self.d_cstate = self.inp("cstate", [DEPTH, 128, 88 * 4 * 2]).ap()
    self.d_cb16 = self.inp("cb16", [128, 4 * 128], BF16).ap()
    self.d_wup = self.inp("w_ffn_up", [nL * D, 2 * DFF]).ap()
    self.d_wdn = self.inp("w_ffn_down", [nL * DFF, D]).ap()
    self.o_yT = self.outp("yT", [D, NT]).ap()
    self.coll = {}
    self.sel = p.sb("sel", [128, 3 * NCORES], F32); self.sel_b = Buf("sel")
    self.d_sel = self.inp("sel", [128, 3 * NCORES]).ap()
    self.ld_sel = p.dsem("ld_sel"); self.ld_x2 = [p.dsem("ld_x2a"), p.dsem("ld_x2b")]
    p.dma(p.sp, self.sel[:], self.d_sel, self.ld_sel, w=[self.sel_b])
    self.fold = p.sb("fold", [128, NCORES], F32)
    self.d_fold = self.inp("fold", [128, NCORES]).ap()
    self.ld_fold = p.dsem("ld_fold")
    p.dma(p.sp, self.fold[:], self.d_fold, self.ld_fold, w=[self.sel_b])
    self.sel_b.w = {id(self.ld_sel.sem): (self.ld_sel.sem, self.ld_sel.cnt), id(self.ld_fold.sem): (self.ld_fold.sem, self.ld_fold.cnt)}
    self.xh_acc = p.sb("xh_acc", [128, 32], F32); self.xh_accb = Buf("xh_acc")
    self.xh_tmp = p.sb("xh_tmp", [128, 2, 32], F32); self.xh_tmpb = [Buf("xh_t0"), Buf("xh_t1")]
    La = [l for l in cfg["layers"] if l % 2 == 0]
    self.attn_idx = {l: i for i, l in enumerate(La)}
    if "attn" in cfg["stages"] and La:
        nA = len(La)
        self.d_wqkv = self.inp("w_qkv", [nA * D, 2560]).ap()
        self.d_wao = self.inp("w_attn_o", [nA * D, D]).ap()
        self.d_cos = self.inp("ropecos", [128, NT]).ap()
        self.d_sin = self.inp("ropesin", [128, NT]).ap()
        self.d_masks = self.inp("masks", [128, 768], BF16).ap()
        self.d_sinks = self.inp("sinks", [nA, 128, NH]).ap()
        self.d_ck = self.inp("ck", [nA, 128, 4, NKV, 128]).ap()
        self.d_cv = self.inp("cv", [nA, 4, 128, 256]).ap()
        self.d_ckn = self.inp("ckn", [nA, 4, 128, 256]).ap()
        self.d_cvn = self.d_cv
        self.o_kp = self.outp("o_kp", [nA, 128, 256]).ap()
        self.o_vp = self.outp("o_vp", [nA, 128, 256]).ap()
        self.o_ks = self.outp("o_ks", [nA, 4, 128, 256]).ap()
        self.o_vs = self.outp("o_vs", [nA, 4, 128, 256]).ap()
        for nm in ("ld_rope", "ld_rope2", "ld_mask", "ld_sink", "ld_ktc", "ld_vcx", "st_vp", "st_kp", "st_ks", "st_vs"):
            setattr(self, nm, p.dsem(nm))
    Ls = [l for l in cfg["layers"] if l % 2 == 1]
    self.ssm_idx = {l: i for i, l in enumerate(Ls)}
    if "ssm" in cfg["stages"] and Ls:
        nS = len(Ls)
        self.d_wsin = self.inp("w_ssm_in", [nS * D, D]).ap()
        self.d_wglu = self.inp("w_ssm_glu", [nS * D, 2 * D]).ap()
        self.d_sprm = self.inp("sprm", [nS, 128, 192]).ap()
        self.d_smask = self.inp("smask", [128, 8]).ap()
        self.d_sd = self.inp("sdsk", [nS, 128, KC]).ap()
        self.d_sst = self.inp("sst", [nS, 2, 128, 256]).ap()
        self.d_sbc = self.inp("sbc", [nS, 4, 128, 1024]).ap()
        self.o_ssp = self.outp("o_ssp", [nS, 2, 128, 64]).ap()
        self.o_sss = self.outp("o_sss", [nS, 2, 128, 256]).ap()
        self.ld_sp = [p.dsem(f"ld_sp{i}") for i in range(7)]
        self.ld_bc = [p.dsem(f"ld_bc{i}") for i in range(4)]
        self.st_ss = [p.dsem(f"st_ss{i}") for i in range(4)]
    if "xattn" in cfg["stages"]:
        self.d_memT = self.inp("memT", [D, NMEM]).ap()
        self.d_cmk = self.inp("cmk", [DEPTH, 4, XH, XD, NMEM]).ap()
        self.d_cmv = self.inp("cmv", [DEPTH, 4, NMEM, XH * XD]).ap()
        self.d_wmk = self.inp("w_mem_k", [nL * D, 512]).ap()
        self.d_wmv = self.inp("w_mem_v", [nL * D, 512]).ap()
        self.d_wxq = self.inp("w_x_q", [nL * D, 512]).ap()
        self.d_wxo = self.inp("w_x_o", [nL * 512, D]).ap()
        self.o_memk = self.outp("o_memk", [DEPTH, NMEM, 512]).ap()
        self.o_memv = self.outp("o_memv", [DEPTH, NMEM, 512]).ap()
        self.ld_kts = p.dsem("ld_kts"); self.ld_vs = p.dsem("ld_vs"); self.ld_mem = p.dsem("ld_mem")
        self.st_mk = p.dsem("st_mk"); self.st_mv = p.dsem("st_mv")
    self.o_conv_p = self.outp("o_conv_p", [DEPTH, 128, 88 * 2]).ap()
    self.o_conv_s = self.outp("o_conv_s", [DEPTH, 128, 88 * 8]).ap()
    xv = self.d_xT.rearrange("(c p) n -> p c n", p=128)
    for c in range(KC):
        p.dma(p.sp, self.X[:, c, 0:NT], xv[:, c, :], self.ld_x, w=[self.xb[c][0], self.xb[c][1], self.xb[c][2]])
    for c in range(KC):
        for t in range(3):
            self.xb[c][t].w = {id(self.ld_x.sem): (self.ld_x.sem, self.ld_x.cnt)}
    p.dma(p.sp, self.gains[:], self.d_gains, self.ld_g, w=[self.gains_b])
    p.dma(p.sp, self.cb16[:].rearrange("p a b -> p (a b)"), self.d_cb16, self.ld_c, w=[self.cb16_b])
    hz = [self.xb[c][3] for c in range(KC)]
    p.op(p.dve, lambda e: e.memset(self.X[:, :, NT:NTH], 0.0), w=hz)
    self.ident = self.cb16[:, 0, :]
    self.ones = self.cb16[:, 1, :]
    self.rot = self.cb16[:, 2, :]


def _gcol(self, l, k, c):
    i = (l * 7 + k) * KC + c
    return self.gains[:, i:i + 1]


def _ssq_rstd(self, srcs, n, sq, sqb, rstd, rstdb, src_bufs):
    p = self.p
    pst, psb = self.psum()
    for c in range(KC):
        i = c % len(sq)
        p.op(p.act, lambda e, c=c, i=i: e.activation(out=sq[i][:, 0:n], in_=srcs[c], func=AF.Square),
             r=src_bufs[c], w=[sqb[i]])
        p.op(p.pe, lambda e, c=c, i=i: e.matmul(pst[:, 0:n], lhsT=self.ones, rhs=sq[i][:, 0:n],
                                                start=(c == 0), stop=(c == KC - 1)),
             r=[sqb[i], self.cb16_b], w=[psb], sig=True)
    p.op(p.dve, lambda e: e.tensor_scalar(out=rstd[:, 0:n], in0=pst[:, 0:n], scalar1=1.0 / D, scalar2=EPS,
                                          op0=ALU.mult, op1=ALU.add), r=[psb], w=[rstdb])
    p.op(p.act, lambda e: e.activation(out=rstd[:, 0:n], in_=rstd[:, 0:n], func=AF.Sqrt), r=[rstdb], w=[rstdb])
    p.op(p.dve, lambda e: e.reciprocal(out=rstd[:, 0:n], in_=rstd[:, 0:n]), r=[rstdb], w=[rstdb])


def _xbufs(self, c, t):
    return [self.xb[c][2], self.xb[c][3]] if t == 2 else [self.xb[c][t]]


def _norm_pre(self, l, gk, items, sq, sqb, rstd, rstdb):
    p = self.p
    for (t, c0, n, dst_fn, dst_buf) in items:
        srcs = [self.X[:, c, c0:c0 + n] for c in range(KC)]
        sbufs = [self.xbufs(c, t) for c in range(KC)]
        self.ssq_rstd(srcs, n, sq, sqb, rstd, rstdb, sbufs)
        for c in range(KC):
            p.op(p.dve, lambda e, c=c: e.scalar_tensor_tensor(out=dst_fn(c), in0=srcs[c], scalar=self.gcol(l, gk, c),
                                                              in1=rstd[:, 0:n], op0=ALU.mult, op1=ALU.mult),
                 r=sbufs[c] + [rstdb, self.gains_b], w=[dst_buf])


def _post_norm(self, l, gk, items, sq, sqb, rstd, rstdb, tmp, tmpb):
    p = self.p
    for (t, c0, n, f_fn, f_bufs) in items:
        srcs = [f_fn(c) for c in range(KC)]
        fb = [f_bufs(c) for c in range(KC)]
        self.ssq_rstd(srcs, n, sq, sqb, rstd, rstdb, fb)
        for c in range(KC):
            i = c % 2
            p.op(p.dve, lambda e, c=c, i=i: e.scalar_tensor_tensor(out=tmp[i][:, 0:n], in0=srcs[c], scalar=self.gcol(l, gk, c),
                                                                   in1=rstd[:, 0:n], op0=ALU.mult, op1=ALU.mult),
                 r=fb[c] + [rstdb, self.gains_b], w=[tmpb[i]])
            xbuf = [self.xb[c][t]]
            p.op(p.dve, lambda e, c=c, i=i: e.tensor_tensor(out=self.X[:, c, c0:c0 + n], in0=self.X[:, c, c0:c0 + n],
                                                            in1=tmp[i][:, 0:n], op=ALU.add),
                 r=[tmpb[i]] + xbuf, w=xbuf)


FFN_PASSES = [[(2, 1024, 34), (0, 0, 512)], [(1, 512, 512)]]


def _plan_ffn(self, l, li):
    for pi in range(2):
        for half in range(2):
            for jj in range(22):
                j = half * 22 + jj

                def fn(slot, issue, j=j):
                    sv = slot[:, 0:4096].rearrange("p (k m) -> p k m", m=256)
                    wa = self.d_wup[li * D:(li + 1) * D, j * 128:(j + 1) * 128].rearrange("(k p) m -> p k m", p=128)
                    wvv = self.d_wup[li * D:(li + 1) * D, DFF + j * 128:DFF + (j + 1) * 128].rearrange("(k p) m -> p k m", p=128)
                    issue(sv[:, :, 0:128], wa)
                    issue(sv[:, :, 128:256], wvv)
                self.ws_jobs.append((("up", l, pi, j), fn))
            for m in range(KC):
                def fn(slot, issue, m=m, half=half):
                    sv = slot[:, 0:22 * 128].rearrange("p (k m) -> p k m", m=128)
                    r0 = li * DFF + half * 2816
                    w = self.d_wdn[r0:r0 + 2816, m * 128:(m + 1) * 128].rearrange("(k p) m -> p k m", p=128)
                    issue(sv, w)
                self.ws_jobs.append((("dn", l, pi, half, m), fn))


def _cw(self, cc, i):
    return self.convw[:, cc * 4 + i:cc * 4 + i + 1]


def _conv_gate(self, l, j, t, n, psa, psab, psv, psvb, act_ap, actbuf, ext, extb, cbuf, cbb):
    p = self.p
    self.cg_par = getattr(self, "cg_par", 0) ^ 1
    cs = []
    for (ps, psb, cc, ei) in ((psa, psab, j, 0), (psv, psvb, FC + j, 1)):
        E = ext[2 * self.cg_par + ei]
        Eb = extb[2 * self.cg_par + ei]
        C = cbuf[2 * self.cg_par + ei]
        Cb = cbb[2 * self.cg_par + ei]
        w0, w1, w2, bb = self.cw(cc, 0), self.cw(cc, 1), self.cw(cc, 2), self.cw(cc, 3)
        if t != 2:
            cin, cinb = (self.carryA, self.carryA_b) if t == 0 else (self.carryB, self.carryB_b)
            cout, coutb = (self.carryB, self.carryB_b) if t == 0 else (self.outc_p, self.outc_p_b)
            p.op(p.dve, lambda e, E=E, cin=cin, cc=cc: e.tensor_copy(out=E[:, 0:2], in_=cin[:, cc, :]), r=[cinb], w=[Eb])
            p.op(p.act, lambda e, E=E, ps=ps: e.activation(out=E[:, 2:2 + n], in_=ps[:, 0:n], func=AF.Copy), r=[psb], w=[Eb])
            p.op(p.dve, lambda e, E=E, cout=cout, cc=cc: e.tensor_copy(out=cout[:, cc, :], in_=E[:, n:n + 2]), r=[Eb], w=[coutb])
            p.op(p.dve, lambda e, E=E, C=C: e.tensor_scalar(out=C[:, 0:n], in0=E[:, 2:2 + n], scalar1=w2, scalar2=bb,
                                                            op0=ALU.mult, op1=ALU.add), r=[Eb, self.convw_b], w=[Cb])
            p.op(p.dve, lambda e, E=E, C=C: e.scalar_tensor_tensor(out=C[:, 0:n], in0=E[:, 1:1 + n], scalar=w1, in1=C[:, 0:n],
                                                                   op0=ALU.mult, op1=ALU.add), r=[Eb, Cb], w=[Cb])
            p.op(p.dve, lambda e, E=E, C=C: e.scalar_tensor_tensor(out=C[:, 0:n], in0=E[:, 0:n], scalar=w0, in1=C[:, 0:n],
                                                                   op0=ALU.mult, op1=ALU.add), r=[Eb, Cb], w=[Cb])
        else:
            E3 = E[:, 0:40].rearrange("p (b j) -> p b j", j=10)
            C3 = C[:, 0:32].rearrange("p (b j) -> p b j", j=8)
            p.op(p.dve, lambda e, E3=E3, cc=cc: e.tensor_copy(out=E3[:, :, 0:2], in_=self.cstate[:, cc, :, :]), r=[self.cstate_b], w=[Eb])
            p.op(p.act, lambda e, E3=E3, ps=ps: e.activation(out=E3[:, :, 2:10], in_=ps[:, 0:32].rearrange("p (b j) -> p b j", j=8),
                                                             func=AF.Copy), r=[psb], w=[Eb])
            p.op(p.dve, lambda e, ps=ps, cc=cc: e.tensor_copy(out=self.carryA[:, cc, :], in_=ps[:, 32:34]), r=[psb], w=[self.carryA_b])
            p.op(p.dve, lambda e, E3=E3, cc=cc: e.tensor_copy(out=self.outc_s[:, cc, :, :], in_=E3[:, :, 8:10]), r=[Eb], w=[self.outc_s_b])
            p.op(p.dve, lambda e, E3=E3, C3=C3: e.tensor_scalar(out=C3, in0=E3[:, :, 2:10], scalar1=w2, scalar2=bb,
                                                                op0=ALU.mult, op1=ALU.add), r=[Eb, self.convw_b], w=[Cb])
            p.op(p.dve, lambda e, E3=E3, C3=C3: e.scalar_tensor_tensor(out=C3, in0=E3[:, :, 1:9], scalar=w1, in1=C3,
                                                                       op0=ALU.mult, op1=ALU.add), r=[Eb, Cb], w=[Cb])
            p.op(p.dve, lambda e, E3=E3, C3=C3: e.scalar_tensor_tensor(out=C3, in0=E3[:, :, 0:8], scalar=w0, in1=C3,
                                                                       op0=ALU.mult, op1=ALU.add), r=[Eb, Cb], w=[Cb])
        cs.append((C, Cb))
    (Ca, Cab), (Cv, Cvb) = cs
    ne = 32 if t == 2 else n
    p.op(p.act, lambda e: e.activation(out=Ca[:, 0:ne], in_=Ca[:, 0:ne], func=AF.Silu), r=[Cab], w=[Cab])
    p.op(p.dve, lambda e: e.tensor_tensor(out=act_ap[:, 0:ne], in0=Ca[:, 0:ne], in1=Cv[:, 0:ne], op=ALU.mult),
         r=[Cab, Cvb], w=[actbuf])
    if t == 2:
        p.op(p.dve, lambda e: e.memset(act_ap[:, 32:34], 0.0), w=[actbuf])


def _ffn(self, l, li):
    p = self.p
    A = self.arena
    if self.cfg.get("exchange"):
        self.exchange_halo()
    A.reset()
    bm0 = Bump(self)
    self.convw = bm0.take(F32, 88 * 4); self.convw_b = A.buf("convw")
    self.cstate = bm0.take(F32, 88, 4, 2); self.cstate_b = A.buf("cstate")
    self.carryA = bm0.take(F32, 88, 2); self.carryA_b = A.buf("cA")
    self.carryB = bm0.take(F32, 88, 2); self.carryB_b = A.buf("cB")
    self.outc_p = bm0.take(F32, 88, 2); self.outc_p_b = A.buf("ocp")
    self.outc_s = bm0.take(F32, 88, 4, 2); self.outc_s_b = A.buf("ocs")
    persist0 = list(A.live)
    p.dma(p.sp, self.convw, self.d_convw[l], self.ld_cw, w=[self.convw_b])
    p.dma(p.sp, self.cstate.rearrange("p a b c -> p (a b c)"), self.d_cstate[l], self.ld_cs, w=[self.cstate_b])
    for pi, ptiles in enumerate(FFN_PASSES):
        A.reset(); A.live.extend(persist0)
        ncol = sum(n for _, _, n in ptiles)
        offs = []
        o = 0
        for (_, _, n) in ptiles:
            offs.append(o)
            o += n
        bm = Bump(self); bm.off = bm0.off
        Hb = bm.take(BF16, KC, ncol)
        hbb = [A.buf(f"hb{i}") for i in range(len(ptiles))]
        act = bm.take(BF16, 22, ncol)
        actb = [[A.buf(f"act{jj}_{i}") for i in range(len(ptiles))] for jj in range(22)]
        f = bm.take(F32, KC, ncol)
        fb = [[A.buf(f"f{m}_{i}") for i in range(len(ptiles))] for m in range(KC)]
        ext = [bm.take(F32, 514) for _ in range(4)]
        extb = A.bufs("ext", 4)
        cbuf = [bm.take(F32, 512) for _ in range(4)]
        cbb = A.bufs("cb", 4)
        sq = [e.bitcast(BF16) for e in ext]
        rstd, rstdb = cbuf[2], cbb[2]
        tmp, tmpb = cbuf[0:2], cbb[0:2]
        items = []
        for i, (t, c0, n) in enumerate(ptiles):
            items.append((t, c0, n, (lambda c, i=i, n=n: Hb[:, c, offs[i]:offs[i] + n]), hbb[i]))
        self.norm_pre(l, 4, items, sq, extb, rstd, rstdb)
        for half in range(2):
            for jj in range(22):
                j = half * 22 + jj
                slot, sb_ = self.ws_get(("up", l, pi, j))
                sv = slot[:, 0:4096].rearrange("p (k m) -> p k m", m=256)
                for i, (t, c0, n) in enumerate(ptiles):
                    o = offs[i]
                    psa, psab = self.psum()
                    psv, psvb = self.psum()
                    for (ps, psb, m0) in ((psa, psab, 0), (psv, psvb, 128)):
                        for k in range(KC):
                            p.op(p.pe, lambda e, ps=ps, k=k, m0=m0, o=o, n=n: e.matmul(
                                ps[:, 0:n], lhsT=sv[:, k, m0:m0 + 128], rhs=Hb[:, k, o:o + n], start=(k == 0), stop=(k == KC - 1)),
                                r=[sb_, hbb[i]], w=[psb], sig=(k == KC - 1))
                    self.conv_gate(l, j, t, n, psa, psab, psv, psvb, act[:, jj, o:o + n], actb[jj][i], ext, extb, cbuf, cbb)
                self.ws_prefetch()
            if pi == 0:
                self.dbg(f"act{half}", act[:, :, 0:34], [actb[jj][0] for jj in range(22)], [128, 22, 34], BF16)
            for m in range(KC):
                slot, sb_ = self.ws_get(("dn", l, pi, half, m))
                sv = slot[:, 0:22 * 128].rearrange("p (k m) -> p k m", m=128)
                for i, (t, c0, n) in enumerate(ptiles):
                    o = offs[i]
                    ps, psb = self.psum()
                    for kk in range(22):
                        p.op(p.pe, lambda e, ps=ps, kk=kk, o=o, n=n: e.matmul(
                            ps[:, 0:n], lhsT=sv[:, kk, :], rhs=act[:, kk, o:o + n], start=(kk == 0), stop=(kk == 21)),
                            r=[sb_, actb[kk][i]], w=[psb], sig=(kk == 21))
                    if half == 0:
                        p.op(p.act, lambda e, ps=ps, m=m, o=o, n=n: e.activation(out=f[:, m, o:o + n], in_=ps[:, 0:n], func=AF.Copy),
                             r=[psb], w=[fb[m][i]])
                    else:
                        p.op(p.dve, lambda e, ps=ps, m=m, o=o, n=n: e.tensor_tensor(out=f[:, m, o:o + n], in0=f[:, m, o:o + n],
                                                                                    in1=ps[:, 0:n], op=ALU.add),
                             r=[psb, fb[m][i]], w=[fb[m][i]])
                self.ws_prefetch()
        if pi == 0:
            self.dbg("f", f[:, :, 0:34], [fb[m][0] for m in range(KC)], [128, KC, 34])
        items = []
        for i, (t, c0, n) in enumerate(ptiles):
            ne = 32 if t == 2 else n
            items.append((t, c0, ne, (lambda c, i=i, ne=ne: f[:, c, offs[i]:offs[i] + ne]), (lambda c, i=i: [fb[c][i]])))
        self.post_norm(l, 5, items, sq, extb, rstd, rstdb, tmp, tmpb)
    p.dma(p.sp, self.o_conv_p[l], self.outc_p.rearrange("p a b -> p (a b)"), self.st_cp, r=[self.outc_p_b])
    p.dma(p.sp, self.o_conv_s[l], self.outc_s.rearrange("p a b c -> p (a b c)"), self.st_cs, r=[self.outc_s_b])


def _dbg(self, name, ap, bufs, shape, dt=F32):
    if name not in self.cfg.get("dbg", ()):
        return
    p = self.p
    o = self.outp("dbg_" + name, shape, dt).ap()
    if not hasattr(self, "dbg_sem"):
        self.dbg_sem = p.dsem("dbgsem")
    p.dma(p.sp, o, ap, self.dbg_sem, r=bufs)


def _finish(self):
    p = self.p
    ov = self.o_yT.rearrange("(c p) n -> p c n", p=128)
    for c in range(KC):
        p.dma(p.sp, ov[:, c, :], self.X[:, c, 0:NT], self.st, r=[self.xb[c][0], self.xb[c][1], self.xb[c][2]])
    for d in p.all_dsems:
        if d.cnt:
            p.wait_tok(p.sp, {id(d.sem): (d.sem, d.cnt)})
    for E in p.engs:
        assert not E.pend, E.name
    assert self.cfg.get("x_stop") or self.ws_next == len(self.ws_jobs), (self.ws_next, len(self.ws_jobs))


def _build(self):
    cfg = self.cfg
    self.setup()
    for li, l in enumerate(cfg["layers"]):
        if "attn" in cfg["stages"] and l % 2 == 0:
            self.plan_attn(l, li)
        if "ssm" in cfg["stages"] and l % 2 == 1:
            self.plan_ssm(l, li)
        if "xattn" in cfg["stages"]:
            self.plan_xattn(l, li)
        if "ffn" in cfg["stages"]:
            self.plan_ffn(l, li)
    for li, l in enumerate(cfg["layers"]):
        if "attn" in cfg["stages"] and l % 2 == 0:
            self.attn(l, li)
        if "ssm" in cfg["stages"] and l % 2 == 1:
            self.ssm(l, li)
        if "xattn" in cfg["stages"]:
            self.xattn(l, li)
        if "ffn" in cfg["stages"]:
            self.ffn(l, li)
    self.finish()


Kernel.wv = _wv
Kernel.setup = _setup
Kernel.gcol = _gcol
Kernel.ssq_rstd = _ssq_rstd
Kernel.xbufs = _xbufs
Kernel.norm_pre = _norm_pre
Kernel.post_norm = _post_norm
Kernel.plan_ffn = _plan_ffn
Kernel.cw = _cw
Kernel.conv_gate = _conv_gate
Kernel.ffn = _ffn
Kernel.finish = _finish
Kernel.dbg = _dbg
Kernel._build = _build


def _bf16():
    import ml_dtypes
    return ml_dtypes.bfloat16


def host_consts():
    bf = _bf16()
    cb = np.zeros((128, 4, 128), np.float32)
    cb[:, 0, :] = np.eye(128)
    cb[:, 1, :] = 1.0
    for pcol in range(128):
        d = pcol % 64
        if d < 32:
            cb[pcol + 32, 2, pcol] = -1.0
        else:
            cb[pcol - 32, 2, pcol] = 1.0
    return cb.reshape(128, 512).astype(bf)


def host_inputs(inp, cfg):
    L = cfg["layers"]
    gk = ["g_mix_pre", "g_mix_post", "g_x_pre", "g_x_post", "g_ffn_pre", "g_ffn_post", "g_mem"]
    gains = np.zeros((128, DEPTH, 7, KC), np.float32)
    for k, name in enumerate(gk):
        gains[:, :, k, :] = inp[name].reshape(DEPTH, KC, 128).transpose(2, 0, 1)
    gains = np.ascontiguousarray(gains.reshape(128, -1))
    cw = np.concatenate([inp["ffn_conv_w"], inp["ffn_conv_b"][:, None, :]], axis=1)
    convw = np.ascontiguousarray(cw.reshape(DEPTH, 4, 88, 128).transpose(0, 3, 2, 1).reshape(DEPTH, 128, 88 * 4))
    cb16 = host_consts()
    wup = np.ascontiguousarray(inp["w_ffn_up"][L].reshape(len(L) * D, 2 * DFF))
    wdn = np.ascontiguousarray(inp["w_ffn_down"][L].reshape(len(L) * DFF, D))
    maps = []
    for c in range(NCORES):
        b, q = c // 4, c % 4
        xp = inp["x_prompt"][b, q * PT:(q + 1) * PT, :]
        xs = inp["x_sample"][4 * c:4 * c + 4].reshape(ST, D)
        xT = np.ascontiguousarray(np.concatenate([xp, xs], axis=0).T)
        cs = inp["state_ffn_conv"][:, 4 * c:4 * c + 4]
        cstate = np.ascontiguousarray(cs.reshape(DEPTH, 4, 2, 88, 128).transpose(0, 4, 3, 1, 2).reshape(DEPTH, 128, 88 * 8))
        m = {"xT": xT, "gains": gains, "convw": convw, "cstate": cstate, "cb16": cb16,
             "w_ffn_up": wup, "w_ffn_down": wdn}
        sel = np.zeros((128, 3 * NCORES), np.float32)
        for m_ in range(3):
            if q - 1 - m_ >= 0:
                sel[:, 8 * m_ + c - 1 - m_] = 1.0
        m["sel"] = sel
        fold = np.zeros((128, NCORES), np.float32)
        fold[:, 4 * b:c] = 1.0
        m["fold"] = fold
        La = [l for l in L if l % 2 == 0]
        if "attn" in cfg["stages"] and La:
            ja = [l // 2 for l in La]
            m["w_qkv"] = np.ascontiguousarray(inp["w_qkv"][ja].reshape(len(ja) * D, 2560))
            m["w_attn_o"] = np.ascontiguousarray(inp["w_attn_o"][ja].reshape(len(ja) * D, D))
            pos = np.concatenate([q * PT + np.arange(PT), np.tile(PAST + np.arange(8), 4)]).astype(np.float32)
            inv = (10000.0 ** (-np.arange(32, dtype=np.float32) * 2.0 / HD)).astype(np.float32)
            ang = pos[None, :] * inv[np.arange(128) % 32][:, None]
            m["ropecos"] = np.cos(ang).astype(np.float32)
            m["ropesin"] = np.sin(ang).astype(np.float32)
            mk = np.zeros((128, 3, 256), np.float32)
            r_ = np.arange(128)[:, None]; s_ = np.arange(256)[None, :]
            band = np.where((s_ > r_) & (s_ <= r_ + 128), 0.0, NEG)
            mk[:, 0] = band
            mk[:, 1] = band if q > 0 else np.where(s_ < 128, NEG, band)
            t_ = np.arange(8)[:, None]; u_ = np.arange(136)[None, :]
            mk[:8, 2, :136] = np.where(((u_ < 128) & (u_ > t_)) | ((u_ >= 128) & (u_ - 128 <= t_)), 0.0, NEG)
            m["masks"] = mk.reshape(128, 768).astype(_bf16())
            m["sinks"] = np.ascontiguousarray(np.broadcast_to(inp["attn_sinks"][ja][:, None, :], (len(ja), 128, NH))).astype(np.float32)
            ck = inp["cache_swa_k"][ja][:, 4 * c:4 * c + 4]
            ckT = ck.transpose(0, 4, 1, 3, 2)
            m["ck"] = np.ascontiguousarray(np.concatenate([ckT, ckT], axis=1))
            m["ckn"] = np.ascontiguousarray(ck.reshape(len(ja), 4, 128, 256))
            m["cv"] = np.ascontiguousarray(inp["cache_swa_v"][ja][:, 4 * c:4 * c + 4].reshape(len(ja), 4, 128, 256))
        Ls = [l for l in L if l % 2 == 1]
        if "ssm" in cfg["stages"] and Ls:
            jsl = [l // 2 for l in Ls]
            nS = len(jsl)
            def SL(a):
                sh = a.shape[2:]
                return np.ascontiguousarray(np.moveaxis(a.reshape((64, 2, 64) + sh), 0, 2).reshape((128, 64) + sh))
            m["w_ssm_in"] = np.ascontiguousarray(inp["w_ssm_in"][jsl].reshape(nS * D, D))
            m["w_ssm_glu"] = np.ascontiguousarray(inp["w_ssm_glu"][jsl].reshape(nS * D, 2 * D))
            sprm = np.zeros((nS, 128, 192), np.float32)
            sbc = np.zeros((nS, 4, 128, 1024), np.float32)
            sst = np.zeros((nS, 2, 128, 256), np.float32)
            sdk = np.zeros((nS, 128, KC), np.float32)
            for i, js in enumerate(jsl):
                sprm[i, :, 0:64] = SL(inp["ssm_lambda_re"][js])
                sprm[i, :, 64:128] = SL(inp["ssm_lambda_im"][js])
                sprm[i, :, 128:192] = SL(np.broadcast_to(inp["ssm_log_step"][js][:, None], (128, 64)))
                sbc[i, 0] = SL(inp["ssm_b_re"][js]).reshape(128, 1024)
                sbc[i, 1] = SL(inp["ssm_b_im"][js]).reshape(128, 1024)
                sbc[i, 2] = SL(inp["ssm_c_re"][js].transpose(0, 2, 1)).reshape(128, 1024)
                sbc[i, 3] = SL(inp["ssm_c_im"][js].transpose(0, 2, 1)).reshape(128, 1024)
                sre = inp["state_ssm_re"][js][4 * c:4 * c + 4].transpose(1, 2, 0)
                sim_ = inp["state_ssm_im"][js][4 * c:4 * c + 4].transpose(1, 2, 0)
                sst[i, 0] = SL(sre).reshape(128, 256)
                sst[i, 1] = SL(sim_).reshape(128, 256)
                sdk[i] = inp["ssm_d"][js].reshape(KC, 128).T
            m["sprm"], m["sbc"], m["sst"], m["sdsk"] = sprm, sbc, sst, sdk
            sm_ = np.zeros((128, 8), np.float32)
            for pp in range(128):
                sm_[pp, pp // 64] = 1.0
                sm_[pp, 2 + pp // 32] = 1.0
            m["smask"] = sm_
        if "xattn" in cfg["stages"]:
            m["memT"] = np.ascontiguousarray(inp["mem_prompt"][b].T)
            m["cmk"] = np.ascontiguousarray(inp["cache_mem_k"][:, 4 * c:4 * c + 4].transpose(0, 1, 3, 4, 2))
            m["cmv"] = np.ascontiguousarray(inp["cache_mem_v"][:, 4 * c:4 * c + 4].reshape(DEPTH, 4, NMEM, 512))
            for nm in ("w_mem_k", "w_mem_v", "w_x_q"):
                m[nm] = np.ascontiguousarray(inp[nm][L].reshape(len(L) * D, 512))
            m["w_x_o"] = np.ascontiguousarray(inp["w_x_o"][L].reshape(len(L) * 512, D))
        maps.append(m)
    return maps


def _plan_cols(self, tag_fn, W2d, Kc, njobs, cols):
    for jb in range(njobs):
        def fn(slot, issue, jb=jb):
            sv = slot[:, 0:Kc * cols].rearrange("p (k m) -> p k m", m=cols)
            issue(sv, W2d[:, jb * cols:(jb + 1) * cols].rearrange("(k p) m -> p k m", p=128))
        self.ws_jobs.append((tag_fn(jb), fn))


def _proj_fm(self, tag_fn, njobs, mc_per_job, Kc, cols, rhs_fn, rhs_bufs, tiles, sink):
    p = self.p
    for jb in range(njobs):
        slot, sb_ = self.ws_get(tag_fn(jb))
        sv = slot[:, 0:Kc * cols].rearrange("p (k m) -> p k m", m=cols)
        for mm in range(mc_per_job):
            m = jb * mc_per_job + mm
            for ti, (t, c0, n) in enumerate(tiles):
                ps, psb = self.psum()
                for k in range(Kc):
                    p.op(p.pe, lambda e, ps=ps, k=k, mm=mm, c0=c0, n=n: e.matmul(
                        ps[:, 0:n], lhsT=sv[:, k, mm * 128:(mm + 1) * 128], rhs=rhs_fn(k, c0, n),
                        start=(k == 0), stop=(k == Kc - 1)),
                        r=[sb_] + rhs_bufs(ti), w=[psb], sig=(k == Kc - 1))
                sink(m, ti, t, c0, n, ps, psb)
        self.ws_prefetch()


def _softmax_rows(self, S, Sb, rows, ncol, scale, E, Eb, P, Pb, sm, smb, extra=None):
    p = self.p
    mx, nb, rs, rr = sm[0:rows, 0:1], sm[0:rows, 1:2], sm[0:rows, 2:3], sm[0:rows, 3:4]
    p.op(p.dve, lambda e: e.tensor_reduce(out=mx, in_=S[0:rows, 0:ncol], axis=AX.X, op=ALU.max), r=[Sb], w=[smb])
    if extra is not None:
        extra("max", rows, sm, smb)
    p.op(p.dve, lambda e: e.tensor_scalar(out=nb, in0=mx, scalar1=-scale, scalar2=None, op0=ALU.mult), r=[smb], w=[smb])
    p.op(p.act, lambda e: e.activation(out=E[0:rows, 0:ncol], in_=S[0:rows, 0:ncol], func=AF.Exp, bias=nb, scale=scale,
                                       accum_out=rs), r=[Sb, smb], w=[Eb, smb])
    if extra is not None:
        extra("sum", rows, sm, smb)
    p.op(p.dve, lambda e: e.reciprocal(out=rr, in_=rs), r=[smb], w=[smb])
    p.op(p.dve, lambda e: e.tensor_scalar(out=P[0:rows, 0:ncol], in0=E[0:rows, 0:ncol], scalar1=rr, scalar2=None,
                                          op0=ALU.mult), r=[Eb, smb], w=[Pb])


X_TILES = [(0, 0, 512), (1, 512, 512), (2, 1024, 32)]


def _plan_xattn(self, l, li):
    self.plan_cols(lambda jb: ("mk", l, jb), self.d_wmk[li * D:(li + 1) * D, :], KC, 2, 256)
    self.plan_cols(lambda jb: ("mv", l, jb), self.d_wmv[li * D:(li + 1) * D, :], KC, 2, 256)
    self.plan_cols(lambda jb: ("xq", l, jb), self.d_wxq[li * D:(li + 1) * D, :], KC, 2, 256)
    for ti in range(3):
        self.plan_cols(lambda jb, ti=ti: ("xo", l, ti, jb), self.d_wxo[li * 512:(li + 1) * 512, :], 4, 2, 1024)


def _xattn(self, l, li):
    p = self.p
    A = self.arena
    scale = float(XD) ** -0.5
    A.reset()
    bm = Bump(self)
    RA = bm.off
    bm.off += KC * NT * 2 + 64
    MKT = bm.take(BF16, XH, NMEM); MKTb = A.buf("mkt")
    MVb = bm.take(BF16, 2, 512); MVbb = A.buf("mvb")
    KTs = bm.take(BF16, 4, XH, NMEM); KTsb = A.buf("kts")
    Vs = bm.take(BF16, 4, 2, 512); Vsb = A.buf("vs")
    QT = bm.take(BF16, XH, NT); QTb = [A.buf(f"qt{i}") for i in range(3)]
    OT = bm.take(BF16, XH, NT); OTb = [A.buf(f"ot{i}") for i in range(3)]
    Eb_ = [bm.take(BF16, 256) for _ in range(2)]; Ebb = A.bufs("E", 2)
    Pb_ = [bm.take(BF16, 256) for _ in range(2)]; Pbb = A.bufs("P", 2)
    PT_ = [bm.take(BF16, 2, 128) for _ in range(2)]; PTb = A.bufs("PT", 2)
    SM = [bm.take(F32, 4) for _ in range(4)]; SMb = A.bufs("sm", 4)
    sqx = [bm.take(BF16, 512) for _ in range(4)]; sqxb = A.bufs("sqx", 4)
    rstd = bm.take(F32, 512); rstdb = A.buf("rstd")
    tmp = [bm.take(F32, 512) for _ in range(2)]; tmpb = A.bufs("tmp", 2)
    persist = list(A.live)
    p.dma(p.pool, KTs, self.d_cmk[l].rearrange("b h d m -> d b h m"), self.ld_kts, w=[KTsb])
    p.dma(p.pool, Vs, self.d_cmv[l].rearrange("b (mc q) f -> q b mc f", q=128), self.ld_vs, w=[Vsb])
    sub = Bump(self); sub.off = RA
    memT = sub.take(F32, KC, NMEM); memTb = A.bufs("memT", KC)
    MN = sub.take(BF16, KC, NMEM); MNb = A.buf("mn")
    stg = [sub.take(F32, 2, 512) for _ in range(2)]; stgb = A.bufs("stg", 2)
    mv_ = self.d_memT.rearrange("(c q) n -> q c n", q=128)
    p.dma(p.sp, memT, mv_, self.ld_mem, w=memTb)
    srcs = [memT[:, c, :] for c in range(KC)]
    self.ssq_rstd(srcs, NMEM, sqx, sqxb, rstd, rstdb, [[b] for b in memTb])
    for c in range(KC):
        p.op(p.dve, lambda e, c=c: e.scalar_tensor_tensor(out=MN[:, c, :], in0=srcs[c], scalar=self.gcol(l, 6, c),
                                                          in1=rstd[:, 0:NMEM], op0=ALU.mult, op1=ALU.mult),
             r=[memTb[c], rstdb, self.gains_b], w=[MNb])
    for which in ("mk", "mv"):
        for jb in range(2):
            slot, sb_ = self.ws_get((which, l, jb))
            sv = slot[:, 0:4096].rearrange("p (k m) -> p k m", m=256)
            if which == "mk":
                for hh in range(2):
                    h = jb * 2 + hh
                    ps, psb = self.psum()
                    for k in range(KC):
                        p.op(p.pe, lambda e, ps=ps, k=k, hh=hh: e.matmul(ps[:, 0:NMEM], lhsT=sv[:, k, hh * 128:(hh + 1) * 128],
                                                                         rhs=MN[:, k, :], start=(k == 0), stop=(k == KC - 1)),
                             r=[sb_, MNb], w=[psb], sig=(k == KC - 1))
                    p.op(p.act, lambda e, ps=ps, h=h: e.activation(out=MKT[:, h, :], in_=ps[:, 0:NMEM], func=AF.Copy), r=[psb], w=[MKTb])
            si = 0 if which == "mk" else 1
            for mc in range(2):
                ps, psb = self.psum()
                for k in range(KC):
                    p.op(p.pe, lambda e, ps=ps, k=k, mc=mc: e.matmul(ps[:, 0:256], lhsT=MN[:, k, mc * 128:(mc + 1) * 128],
                                                                     rhs=sv[:, k, :], start=(k == 0), stop=(k == KC - 1)),
                         r=[sb_, MNb], w=[psb], sig=(k == KC - 1))
                p.op(p.act, lambda e, ps=ps, mc=mc, jb=jb, si=si: e.activation(out=stg[si][:, mc, jb * 256:(jb + 1) * 256], in_=ps[:, 0:256],
                                                                               func=AF.Copy), r=[psb], w=[stgb[si]])
                if which == "mv":
                    p.op(p.dve, lambda e, ps=ps, mc=mc, jb=jb: e.tensor_copy(out=MVb[:, mc, jb * 256:(jb + 1) * 256], in_=ps[:, 0:256]),
                         r=[psb], w=[MVbb])
            self.ws_prefetch()
        od, osem = (self.o_memk, self.st_mk) if which == "mk" else (self.o_memv, self.st_mv)
        p.dma(p.sp, od[l].rearrange("(mc q) f -> q mc f", q=128), stg[si], osem, r=[stgb[si]])
    if self.cfg.get("x_stop", 9) <= 1:
        return
    A.reset(); A.live.extend(persist)
    sub = Bump(self); sub.off = RA
    Hb = sub.take(BF16, KC, NT); hbb = [A.buf(f"hb{i}") for i in range(3)]
    items = [(t, c0, n, (lambda c, c0=c0, n=n: Hb[:, c, c0:c0 + n]), hbb[i]) for i, (t, c0, n) in enumerate(X_TILES)]
    self.norm_pre(l, 2, items, sqx, sqxb, rstd, rstdb)

    def qsink(m, ti, t, c0, n, ps, psb):
        p.op(p.act, lambda e: e.activation(out=QT[:, m, c0:c0 + n], in_=ps[:, 0:n], func=AF.Copy), r=[psb], w=[QTb[ti]])
    self.proj_fm(lambda jb: ("xq", l, jb), 2, 2, KC, 256, lambda k, c0, n: Hb[:, k, c0:c0 + n], lambda ti: [hbb[ti]], X_TILES, qsink)
    if self.cfg.get("x_stop", 9) <= 2:
        return
    u = 0
    for blk in range(PT // 128):
        ti = blk // 4
        q0 = blk * 128
        for h in range(XH):
            i2 = u % 2
            smi = u % 4
            u += 1
            S, Sb = self.psum()
            p.op(p.pe, lambda e, S=S, h=h, q0=q0: e.matmul(S[:, 0:NMEM], lhsT=QT[:, h, q0:q0 + 128], rhs=MKT[:, h, :], start=True, stop=True),
                 r=[QTb[ti], MKTb], w=[Sb])
            self.softmax_rows(S, Sb, 128, NMEM, scale, Eb_[i2], Ebb[i2], Pb_[i2], Pbb[i2], SM[smi], SMb[smi])
            T, Tb = self.psum()
            Tv = T[:].bitcast(BF16)
            for mc in range(2):
                p.op(p.pe, lambda e, Tv=Tv, mc=mc, i2=i2: e.transpose(out=Tv[:, mc * 128:(mc + 1) * 128], in_=Pb_[i2][:, mc * 128:(mc + 1) * 128],
                                                                      identity=self.ident), r=[Pbb[i2], self.cb16_b], w=[Tb], sig=(mc == 1))
            p.op(p.act, lambda e, Tv=Tv, i2=i2: e.activation(out=PT_[i2].rearrange("p a b -> p (a b)"), in_=Tv[:, 0:256], func=AF.Copy),
                 r=[Tb], w=[PTb[i2]])
            O, Ob = self.psum()
            for mc in range(2):
                p.op(p.pe, lambda e, O=O, mc=mc, h=h, i2=i2: e.matmul(O[:, 0:128], lhsT=MVb[:, mc, h * 128:(h + 1) * 128], rhs=PT_[i2][:, mc, :],
                                                                      start=(mc == 0), stop=(mc == 1)), r=[MVbb, PTb[i2]], w=[Ob], sig=(mc == 1))
            p.op(p.dve, lambda e, O=O, h=h, q0=q0: e.tensor_copy(out=OT[:, h, q0:q0 + 128], in_=O[:, 0:128]), r=[Ob], w=[OTb[ti]])
    for b in range(4 if self.cfg.get("x_stop", 9) > 3 else 0):
        q0 = PT + 8 * b
        for h in range(XH):
            i2 = u % 2
            smi = u % 4
            u += 1
            S, Sb = self.psum()
            p.op(p.pe, lambda e, S=S, h=h, q0=q0, b=b: e.matmul(S[0:8, 0:NMEM], lhsT=QT[:, h, q0:q0 + 8], rhs=KTs[:, b, h, :], start=True, stop=True),
                 r=[QTb[2], KTsb], w=[Sb])
            self.softmax_rows(S, Sb, 8, NMEM, scale, Eb_[i2], Ebb[i2], Pb_[i2], Pbb[i2], SM[smi], SMb[smi])
            T, Tb = self.psum()
            Tv = T[:].bitcast(BF16)
            for mc in range(2):
                p.op(p.pe, lambda e, Tv=Tv, mc=mc, i2=i2: e.transpose(out=Tv[:, mc * 8:(mc + 1) * 8], in_=Pb_[i2][0:8, mc * 128:(mc + 1) * 128],
                                                                      identity=self.ident[0:8, 0:8]), r=[Pbb[i2], self.cb16_b], w=[Tb], sig=(mc == 1))
            p.op(p.act, lambda e, Tv=Tv, i2=i2: e.activation(out=PT_[i2][:, 0, 0:16], in_=Tv[:, 0:16], func=AF.Copy), r=[Tb], w=[PTb[i2]])
            O, Ob = self.psum()
            for mc in range(2):
                p.op(p.pe, lambda e, O=O, mc=mc, h=h, i2=i2, b=b: e.matmul(O[:, 0:8], lhsT=Vs[:, b, mc, h * 128:(h + 1) * 128],
                                                                           rhs=PT_[i2][:, 0, mc * 8:(mc + 1) * 8],
                                                                           start=(mc == 0), stop=(mc == 1)), r=[Vsb, PTb[i2]], w=[Ob], sig=(mc == 1))
            p.op(p.dve, lambda e, O=O, h=h, q0=q0: e.tensor_copy(out=OT[:, h, q0:q0 + 8], in_=O[:, 0:8]), r=[Ob], w=[OTb[2]])
    if self.cfg.get("x_stop", 9) <= 4:
        return
    A.reset(); A.live.extend(persist)
    sub = Bump(self); sub.off = RA
    f = sub.take(F32, KC, 512); fb = A.bufs("f", KC)
    for ti, (t, c0, n) in enumerate(X_TILES):
        def osink(m, ti_, t_, c0_, n_, ps, psb):
            eng = p.act if m % 2 == 0 else p.dve
            if m % 2 == 0:
                p.op(p.act, lambda e: e.activation(out=f[:, m, 0:n_], in_=ps[:, 0:n_], func=AF.Copy), r=[psb], w=[fb[m]])
            else:
                p.op(p.dve, lambda e: e.tensor_copy(out=f[:, m, 0:n_], in_=ps[:, 0:n_]), r=[psb], w=[fb[m]])
        self.proj_fm(lambda jb, ti=ti: ("xo", l, ti, jb), 2, 8, 4, 1024, lambda k, c0_, n_: OT[:, k, c0_:c0_ + n_], lambda ti_: [OTb[ti]],
                     [(t, c0, n)], osink)
        self.post_norm(l, 3, [(t, c0, n, (lambda c, n=n: f[:, c, 0:n]), (lambda c: [fb[c]]))], sqx, sqxb, rstd, rstdb, tmp, tmpb)


Kernel.plan_cols = _plan_cols
Kernel.proj_fm = _proj_fm
Kernel.softmax_rows = _softmax_rows
Kernel.plan_xattn = _plan_xattn
Kernel.xattn = _xattn


def _exchange(self, name, payload, pbufs, ncol, dt):
    raise NotImplementedError


def _plan_attn(self, l, li):
    j = self.attn_idx[l]
    Wq = self.d_wqkv[j * D:(j + 1) * D, :]
    Wo = self.d_wao[j * D:(j + 1) * D, :]
    for ti in range(3):
        for half in range(2):
            def fnk(slot, issue, half=half):
                sv = slot[:, 0:KC * 256].rearrange("p (k a b c) -> p k a b c", a=2, b=2, c=64)
                for kvi in range(2):
                    kv = half * 2 + kvi
                    src = Wq[:, 2048 + kv * 64:2048 + (kv + 1) * 64].rearrange("(k p) m -> p k m", p=128)
                    for dup in range(2):
                        issue(sv[:, :, kvi, dup, :], src)
            self.ws_jobs.append((("wk", l, ti, half), fnk))
        self.plan_cols(lambda jb, ti=ti: ("wv", l, ti, jb), Wq[:, 2304:2560], KC, 1, 256)
    for ti in range(3):
        self.plan_cols(lambda jb, ti=ti: ("wq", l, ti, jb), Wq[:, 0:2048], KC, 8, 256)
        self.plan_cols(lambda jb, ti=ti: ("wo", l, ti, jb), Wo, KC, 8, 256)


A_TILES = [(2, 1024, 32), (0, 0, 512), (1, 512, 512)]


def _rope(self, ps, psb, n, c0, dst, dstb, sc):
    p = self.p
    i = sc["i"] = sc["i"] ^ 1
    qb, qbb = sc["qb"][i], sc["qbb"][i]
    t1, t1b = sc["t1"][i], sc["t1b"][i]
    t2, t2b = sc["t2"][i], sc["t2b"][i]
    p.op(p.act, lambda e: e.activation(out=qb[:, 0:n], in_=ps[:, 0:n], func=AF.Copy), r=[psb], w=[qbb])
    rp, rpb = self.psum()
    p.op(p.pe, lambda e: e.matmul(rp[:, 0:n], lhsT=self.rot, rhs=qb[:, 0:n], start=True, stop=True), r=[qbb, self.cb16_b], w=[rpb])
    p.op(p.dve, lambda e: e.tensor_tensor(out=t1[:, 0:n], in0=qb[:, 0:n], in1=self.cosT[:, 0:n], op=ALU.mult), r=[qbb, self.rope_b], w=[t1b])
    p.op(p.dve, lambda e: e.tensor_tensor(out=t2[:, 0:n], in0=rp[:, 0:n], in1=self.sinT[:, 0:n], op=ALU.mult), r=[rpb, self.rope_b], w=[t2b])
    p.op(p.dve, lambda e: e.tensor_tensor(out=dst, in0=t1[:, 0:n], in1=t2[:, 0:n], op=ALU.add), r=[t1b, t2b], w=[dstb])


def _attn(self, l, li):
    p = self.p
    A = self.arena
    j = self.attn_idx[l]
    scale = float(HD) ** -0.5
    A.reset()
    bm = Bump(self)
    KOFF = 128
    KT = bm.take(BF16, NKV, KOFF + NT); KTb = [A.buf(f"kt{i}") for i in range(4)]
    VX = bm.take(BF16, 9, NKV, 128); VXb = [A.buf(f"vx{i}") for i in range(9)]
    KTc = bm.take(BF16, 4, NKV, 136); KTcb = A.buf("ktc")
    VCX = bm.take(BF16, 4, NKV, 128); VCXb = A.buf("vcx")
    VNX = bm.take(BF16, 4, NKV, 128); VNXb = A.buf("vnx")
    cosT = bm.take(F32, 512); sinT = bm.take(F32, 512); self.cosT, self.sinT = cosT, sinT; self.rope_b = A.buf("rope")
    masks = bm.take(BF16, 3, 256); maskb = A.buf("masks")
    sinkr = bm.take(F32, 2, NH); sinkb = A.buf("sink")
    sc = {"i": 0, "qb": [bm.take(BF16, 512) for _ in range(2)], "qbb": A.bufs("qb", 2)}
    _t1 = bm.take(F32, 512); _t1b = A.buf("t1"); _t2 = bm.take(F32, 512); _t2b = A.buf("t2")
    sc.update({"t1": [_t1, _t1], "t1b": [_t1b, _t1b], "t2": [_t2, _t2], "t2b": [_t2b, _t2b]})
    E_ = [bm.take(BF16, 256) for _ in range(2)]; Ebb = A.bufs("E", 2)
    P_ = [bm.take(BF16, 256) for _ in range(2)]; Pbb = A.bufs("P", 2)
    PT_ = [bm.take(BF16, 2, 128) for _ in range(2)]; PTb = A.bufs("PT", 2)
    SM = [bm.take(F32, 4) for _ in range(4)]; SMb = A.bufs("sm", 4)
    sq = [bm.take(BF16, 512) for _ in range(2)]; sqb = A.bufs("sq", 2)
    rstd = bm.take(F32, 512); rstdb = A.buf("rstd")
    tmp = [_t1, _t2]; tmpb = [_t1b, _t2b]
    stgv = bm.take(F32, 256); stgvb = A.buf("stgv")
    stgk = bm.take(F32, 256); stgkb = A.buf("stgk")
    stgs = bm.take(F32, 4, 256); stgsb = A.buf("stgs")
    persist = list(A.live)
    RA = bm.off
    self.xk_off = RA
    def load_rope(c0, n):
        p.dma(p.sp, cosT[:, 0:n], self.d_cos[:, c0:c0 + n], self.ld_rope, w=[self.rope_b])
        p.dma(p.sp, sinT[:, 0:n], self.d_sin[:, c0:c0 + n], self.ld_rope2, w=[self.rope_b])
        self.rope_b.w = {id(self.ld_rope.sem): (self.ld_rope.sem, self.ld_rope.cnt), id(self.ld_rope2.sem): (self.ld_rope2.sem, self.ld_rope2.cnt)}
    p.dma(p.sp, masks.rearrange("p a b -> p (a b)"), self.d_masks, self.ld_mask, w=[maskb])
    p.dma(p.sp, sinkr[:, 0, :], self.d_sinks[j], self.ld_sink, w=[sinkb])
    p.op(p.dve, lambda e: e.tensor_scalar(out=sinkr[:, 1, :], in0=sinkr[:, 0, :], scalar1=8.0, scalar2=None, op0=ALU.mult), r=[sinkb], w=[sinkb])
    for b in range(4):
        p.dma(p.pool, KTc[:, b, :, 0:128], self.d_ck[j][:, b, :, :], self.ld_ktc, w=[KTcb])
    p.op(p.dve, lambda e: e.memset(VCX.rearrange("p a b c -> p (a b c)"), 0.0), w=[VCXb])
    p.op(p.dve, lambda e: e.memset(VNX.rearrange("p a b c -> p (a b c)"), 0.0), w=[VNXb])
    p.op(p.dve, lambda e: e.memset(VX.rearrange("p a b c -> p (a b c)"), 0.0), w=VXb)
    p.op(p.dve, lambda e: e.memset(KT[:, :, 0:KOFF], 0.0), w=[KTb[0]])
    for b in range(4):
        for dup in range(2):
            p.dma(p.pool, VCX[:, b, :, dup * 64:(dup + 1) * 64], self.d_cv[j][b].rearrange("s (k d) -> s k d", d=64), self.ld_vcx, w=[VCXb])
    sub = Bump(self); sub.off = RA
    Hb = [sub.take(BF16, KC, 512) for _ in range(2)]; hbb = A.bufs("hb", 2)
    for ti, (t, c0, n) in enumerate(A_TILES):
        hi = ti % 2
        load_rope(c0, n)
        self.norm_pre(l, 0, [(t, c0, n, (lambda c, hi=hi, n=n: Hb[hi][:, c, 0:n]), hbb[hi])], sq, sqb, rstd, rstdb)
        kbuf = KTb[3] if t == 2 else KTb[1 + t]
        for half in range(2):
            slot, sb_ = self.ws_get(("wk", l, ti, half))
            sv = slot[:, 0:KC * 256].rearrange("p (k m) -> p k m", m=256)
            for kvi in range(2):
                kv = half * 2 + kvi
                ps, psb = self.psum()
                for k in range(KC):
                    p.op(p.pe, lambda e: e.matmul(ps[:, 0:n], lhsT=sv[:, k, kvi * 128:(kvi + 1) * 128], rhs=Hb[hi][:, k, 0:n],
                                                  start=(k == 0), stop=(k == KC - 1)), r=[sb_, hbb[hi]], w=[psb], sig=(k == KC - 1))
                self.rope(ps, psb, n, c0, KT[:, kv, KOFF + c0:KOFF + c0 + n], kbuf, sc)
            self.ws_prefetch()
        slot, sb_ = self.ws_get(("wv", l, ti, 0))
        sv = slot[:, 0:KC * 256].rearrange("p (k m) -> p k m", m=256)
        if t != 2:
            for bl in range(4):
                blk = 1 + t * 4 + bl
                ps, psb = self.psum()
                for k in range(KC):
                    p.op(p.pe, lambda e: e.matmul(ps[:, 0:256], lhsT=Hb[hi][:, k, bl * 128:(bl + 1) * 128], rhs=sv[:, k, :],
                                                  start=(k == 0), stop=(k == KC - 1)), r=[sb_, hbb[hi]], w=[psb], sig=(k == KC - 1))
                p.op(p.act, lambda e: e.activation(out=VX[:, blk, :, 0:64], in_=ps[:, 0:256].rearrange("p (k d) -> p k d", d=64), func=AF.Copy),
                     r=[psb], w=[VXb[blk]])
                p.op(p.dve, lambda e: e.tensor_copy(out=VX[:, blk, :, 64:128], in_=ps[:, 0:256].rearrange("p (k d) -> p k d", d=64)),
                     r=[psb], w=[VXb[blk]])
                if blk == 8:
                    p.op(p.dve, lambda e: e.tensor_copy(out=stgv[:, :], in_=ps[:, 0:256]), r=[psb], w=[stgvb])
                    p.dma(p.sp, self.o_vp[j], stgv, self.st_vp, r=[stgvb])
        else:
            for b in range(4):
                ps, psb = self.psum()
                for k in range(KC):
                    p.op(p.pe, lambda e: e.matmul(ps[0:8, 0:256], lhsT=Hb[hi][:, k, 8 * b:8 * b + 8], rhs=sv[:, k, :],
                                                  start=(k == 0), stop=(k == KC - 1)), r=[sb_, hbb[hi]], w=[psb], sig=(k == KC - 1))
                p.op(p.act, lambda e: e.activation(out=VNX[0:8, b, :, 0:64], in_=ps[0:8, 0:256].rearrange("p (k d) -> p k d", d=64), func=AF.Copy),
                     r=[psb], w=[VNXb])
                p.op(p.dve, lambda e: e.tensor_copy(out=VNX[0:8, b, :, 64:128], in_=ps[0:8, 0:256].rearrange("p (k d) -> p k d", d=64)),
                     r=[psb], w=[VNXb])
                p.op(p.dve, lambda e: e.tensor_copy(out=stgs[0:8, b, :], in_=ps[0:8, 0:256]), r=[psb], w=[stgsb])
        self.ws_prefetch()
    for kv in range(NKV):
        T, Tb = self.psum()
        Tv = T[:].bitcast(BF16)
        p.op(p.pe, lambda e: e.transpose(out=Tv[:, 0:64], in_=KT[0:64, kv, KOFF + PT - 128:KOFF + PT], identity=self.ident[0:64, 0:64]),
             r=[KTb[2], self.cb16_b], w=[Tb])
        p.op(p.dve, lambda e: e.tensor_copy(out=stgk[:, kv * 64:(kv + 1) * 64], in_=Tv[:, 0:64]), r=[Tb], w=[stgkb])
        T2, T2b = self.psum()
        T2v = T2[:].bitcast(BF16)
        p.op(p.pe, lambda e: e.transpose(out=T2v[0:32, 0:64], in_=KT[0:64, kv, KOFF + PT:KOFF + NT], identity=self.ident[0:64, 0:64]),
             r=[KTb[3], self.cb16_b], w=[T2b])
        p.op(p.dve, lambda e: e.tensor_copy(out=self.ks32[0:32, kv * 64:(kv + 1) * 64], in_=T2v[0:32, 0:64]), r=[T2b], w=[self.ks32_b])
        p.op(p.dve, lambda e: e.tensor_copy(out=KTc[:, :, kv, 128:136], in_=KT[:, kv, KOFF + PT:KOFF + NT].rearrange("p (b t) -> p b t", t=8)),
             r=[KTb[3]], w=[KTcb])
    p.dma(p.sp, self.o_kp[j], stgk, self.st_kp, r=[stgkb])
    p.dma(p.sp, self.o_ks[j][:, 0:120, :], self.d_ckn[j][:, 8:128, :], self.st_ks, r=[])
    p.dma(p.sp, self.o_vs[j][:, 0:120, :], self.d_cvn[j][:, 8:128, :], self.st_vs, r=[])
    for b in range(4):
        p.dma(p.sp, self.o_ks[j][b, 120:128, :], self.ks32[8 * b:8 * b + 8, :], self.st_ks, r=[self.ks32_b])
    p.dma(p.sp, self.o_vs[j][:, 120:128, :].rearrange("b t f -> t b f"), stgs[0:8, :, :], self.st_vs, r=[stgsb])
    if self.cfg.get("exchange"):
        self.exchange_kv(KT, KTb, VX, VXb, KOFF)
    for ti, (t, c0, n) in enumerate(A_TILES):
        A.reset(); A.live.extend(persist)
        sub = Bump(self); sub.off = RA
        Hq = sub.take(BF16, KC, 512); hqb = A.buf("hq")
        QT = sub.take(BF16, KC, 512)
        nblk = 4 if t != 2 else 4
        QTb = [[A.buf(f"q{c}_{bq}") for bq in range(4)] for c in range(KC)]
        load_rope(c0, n)
        self.norm_pre(l, 0, [(t, c0, n, (lambda c, n=n: Hq[:, c, 0:n]), hqb)], sq, sqb, rstd, rstdb)

        for jb in range(8):
            slot, sb_ = self.ws_get(("wq", l, ti, jb))
            sv = slot[:, 0:KC * 256].rearrange("p (k m) -> p k m", m=256)
            for mm in range(2):
                m = jb * 2 + mm
                ps, psb = self.psum()
                for k in range(KC):
                    p.op(p.pe, lambda e: e.matmul(ps[:, 0:n], lhsT=sv[:, k, mm * 128:(mm + 1) * 128], rhs=Hq[:, k, 0:n],
                                                  start=(k == 0), stop=(k == KC - 1)), r=[sb_, hqb], w=[psb], sig=(k == KC - 1))
                self.rope_multi(ps, psb, n, c0, QT[:, m, 0:n], QTb[m], sc)
            self.ws_prefetch()
        u = 0
        if t != 2:
            for bl in range(4):
                blk = 1 + t * 4 + bl
                q0 = bl * 128
                k0 = KOFF + c0 + q0 - 128
                mi = 1 if (t == 0 and bl == 0) else 0
                kbufs = [KTb[0], KTb[1]] if t == 0 else [KTb[1], KTb[2]]
                for c in range(KC):
                    kv = c // 4
                    for par in range(2):
                        h = 2 * c + par
                        hp = 64 * par
                        i2 = u % 2
                        smi = u % 4
                        u += 1
                        O, Ob = self.psum()
                        S, Sb = self.psum()
                        p.op(p.pe, lambda e: e.matmul(S[:, 0:256], lhsT=QT[hp:hp + 64, c, q0:q0 + 128], rhs=KT[hp:hp + 64, kv, k0:k0 + 256],
                                                      start=True, stop=False), r=[QTb[c][bl]] + kbufs, w=[Sb], sig=False)
                        p.op(p.pe, lambda e: e.matmul(S[:, 0:256], lhsT=self.ident, rhs=masks[:, mi, :], start=False, stop=True),
                             r=[maskb, self.cb16_b], w=[Sb])
                        self.softmax_rows(S, Sb, 128, 256, scale, E_[i2], Ebb[i2], P_[i2], Pbb[i2], SM[smi], SMb[smi],
                                          extra=self.sink_hook(sinkr, sinkb, h))
                        T, Tb = self.psum()
                        Tv = T[:].bitcast(BF16)
                        for mc in range(2):
                            p.op(p.pe, lambda e: e.transpose(out=Tv[:, mc * 128:(mc + 1) * 128], in_=P_[i2][:, mc * 128:(mc + 1) * 128],
                                                             identity=self.ident), r=[Pbb[i2], self.cb16_b], w=[Tb], sig=(mc == 1))
                        p.op(p.act, lambda e: e.activation(out=PT_[i2].rearrange("p a b -> p (a b)"), in_=Tv[:, 0:256], func=AF.Copy),
                             r=[Tb], w=[PTb[i2]])
                        for mc in range(2):
                            vb = blk - 1 + mc
                            p.op(p.pe, lambda e: e.matmul(O[:, 0:128], lhsT=VX[:, vb, kv, :], rhs=PT_[i2][:, mc, :],
                                                          start=(mc == 0), stop=(mc == 1)),
                                 r=[VXb[vb], PTb[i2]], w=[Ob], sig=(mc == 1))
                        p.op(p.dve, lambda e: e.tensor_copy(out=QT[hp:hp + 64, c, q0:q0 + 128], in_=O[hp:hp + 64, 0:128]), r=[Ob], w=[QTb[c][bl]])
        else:
            for b in range(4):
                q0 = 8 * b
                for c in range(KC):
                    kv = c // 4
                    for par in range(2):
                        h = 2 * c + par
                        hp = 64 * par
                        i2 = u % 2
                        smi = u % 4
                        u += 1
                        O, Ob = self.psum()
                        S, Sb = self.psum()
                        p.op(p.pe, lambda e: e.matmul(S[0:8, 0:136], lhsT=QT[hp:hp + 64, c, q0:q0 + 8], rhs=KTc[hp:hp + 64, b, kv, :],
                                                      start=True, stop=False), r=[QTb[c][0], KTcb], w=[Sb], sig=False)
                        p.op(p.pe, lambda e: e.matmul(S[0:8, 0:136], lhsT=self.ident[0:8, 0:8], rhs=masks[0:8, 2, 0:136], start=False, stop=True),
                             r=[maskb, self.cb16_b], w=[Sb])
                        self.softmax_rows(S, Sb, 8, 136, scale, E_[i2], Ebb[i2], P_[i2], Pbb[i2], SM[smi], SMb[smi],
                                          extra=self.sink_hook(sinkr, sinkb, h))
                        T, Tb = self.psum()
                        Tv = T[:].bitcast(BF16)
                        p.op(p.pe, lambda e: e.transpose(out=Tv[:, 0:8], in_=P_[i2][0:8, 0:128], identity=self.ident[0:8, 0:8]),
                             r=[Pbb[i2], self.cb16_b], w=[Tb], sig=False)
                        p.op(p.pe, lambda e: e.transpose(out=Tv[0:8, 8:16], in_=P_[i2][0:8, 128:136], identity=self.ident[0:8, 0:8]),
                             r=[Pbb[i2], self.cb16_b], w=[Tb])
                        p.op(p.act, lambda e: e.activation(out=PT_[i2][:, 0, 0:8], in_=Tv[:, 0:8], func=AF.Copy), r=[Tb], w=[PTb[i2]])
                        p.op(p.act, lambda e: e.activation(out=PT_[i2][0:8, 0, 8:16], in_=Tv[0:8, 8:16], func=AF.Copy), r=[Tb], w=[PTb[i2]])
                        p.op(p.pe, lambda e: e.matmul(O[:, 0:8], lhsT=VCX[:, b, kv, :], rhs=PT_[i2][:, 0, 0:8],
                                                      start=True, stop=False), r=[VCXb, PTb[i2]], w=[Ob], sig=False)
                        p.op(p.pe, lambda e: e.matmul(O[:, 0:8], lhsT=VNX[0:8, b, kv, :], rhs=PT_[i2][0:8, 0, 8:16],
                                                      start=False, stop=True), r=[VNXb, PTb[i2]], w=[Ob], sig=True)
                        p.op(p.dve, lambda e: e.tensor_copy(out=QT[hp:hp + 64, c, q0:q0 + 8], in_=O[hp:hp + 64, 0:8]), r=[Ob], w=[QTb[c][0]])
        f = self.wv(RA, F32, 8, 512)
        f2 = sub.take(F32, 8, 512)
        fb = [Buf(f"f{m}", hqb.hazards()) if m < 8 else A.buf(f"f{m}") for m in range(KC)]
        fv = lambda m: (f[:, m, :] if m < 8 else f2[:, m - 8, :])
        qall = [b_ for row in QTb for b_ in row]
        for jb in range(8):
            slot, sb_ = self.ws_get(("wo", l, ti, jb))
            sv = slot[:, 0:KC * 256].rearrange("p (k m) -> p k m", m=256)
            for mm in range(2):
                m = jb * 2 + mm
                ps, psb = self.psum()
                for k in range(KC):
                    p.op(p.pe, lambda e: e.matmul(ps[:, 0:n], lhsT=sv[:, k, mm * 128:(mm + 1) * 128], rhs=QT[:, k, 0:n],
                                                  start=(k == 0), stop=(k == KC - 1)), r=[sb_] + QTb[k], w=[psb], sig=(k == KC - 1))
                if m % 2 == 0:
                    p.op(p.act, lambda e: e.activation(out=fv(m)[:, 0:n], in_=ps[:, 0:n], func=AF.Copy), r=[psb], w=[fb[m]])
                else:
                    p.op(p.dve, lambda e: e.tensor_copy(out=fv(m)[:, 0:n], in_=ps[:, 0:n]), r=[psb], w=[fb[m]])
            self.ws_prefetch()
        self.post_norm(l, 1, [(t, c0, n, (lambda c, n=n: fv(c)[:, 0:n]), (lambda c: [fb[c]]))], sq, sqb, rstd, rstdb, tmp, tmpb)
        A.live.extend(b_ for b_ in fb if b_ not in A.live)


def _rope_multi(self, ps, psb, n, c0, dst, dstbufs, sc):
    p = self.p
    i = sc["i"] = sc["i"] ^ 1
    qb, qbb = sc["qb"][i], sc["qbb"][i]
    t1, t1b = sc["t1"][i], sc["t1b"][i]
    t2, t2b = sc["t2"][i], sc["t2b"][i]
    p.op(p.act, lambda e: e.activation(out=qb[:, 0:n], in_=ps[:, 0:n], func=AF.Copy), r=[psb], w=[qbb])
    rp, rpb = self.psum()
    p.op(p.pe, lambda e: e.matmul(rp[:, 0:n], lhsT=self.rot, rhs=qb[:, 0:n], start=True, stop=True), r=[qbb, self.cb16_b], w=[rpb])
    p.op(p.dve, lambda e: e.tensor_tensor(out=t1[:, 0:n], in0=qb[:, 0:n], in1=self.cosT[:, 0:n], op=ALU.mult), r=[qbb, self.rope_b], w=[t1b])
    p.op(p.dve, lambda e: e.tensor_tensor(out=t2[:, 0:n], in0=rp[:, 0:n], in1=self.sinT[:, 0:n], op=ALU.mult), r=[rpb, self.rope_b], w=[t2b])
    p.op(p.dve, lambda e: e.tensor_tensor(out=dst, in0=t1[:, 0:n], in1=t2[:, 0:n], op=ALU.add), r=[t1b, t2b], w=list(dstbufs))


def _rope1(self, ps, psb, n, c0, dst, dstb, sc):
    return _rope_multi(self, ps, psb, n, c0, dst, [dstb], sc)


def _sink_hook(self, sinkr, sinkb, h):
    p = self.p

    def hook(stage, rows, sm, smb):
        mx, nb, rs = sm[0:rows, 0:1], sm[0:rows, 1:2], sm[0:rows, 2:3]
        es = sm[0:rows, 3:4]
        if stage == "max":
            p.op(p.dve, lambda e: e.tensor_tensor(out=mx, in0=mx, in1=sinkr[0:rows, 1, h:h + 1], op=ALU.max), r=[smb, sinkb], w=[smb])
        else:
            p.op(p.act, lambda e: e.activation(out=es, in_=nb, func=AF.Exp, bias=sinkr[0:rows, 0, h:h + 1], scale=1.0), r=[smb, sinkb], w=[smb])
            p.op(p.dve, lambda e: e.tensor_tensor(out=rs, in0=rs, in1=es, op=ALU.add), r=[smb], w=[smb])
    return hook


Kernel.plan_attn = _plan_attn
Kernel.attn = _attn
Kernel.rope = _rope1
Kernel.rope_multi = _rope_multi
Kernel.sink_hook = _sink_hook


def _coll_gather(self, name, parts, ncol, dt):
    p = self.p
    if name not in self.coll:
        ib = self.nc.dram_tensor("ib_" + name, [128, ncol], dt)
        ob = self.nc.dram_tensor("ob_" + name, [NCORES * 128, ncol], dt)
        if not hasattr(self, "cc_all"):
            self.cc_all = DSem(p.sem("cc_all"))
        self.coll[name] = (ib, ob, Buf("ib_" + name), Buf("ob_" + name), p.dsem("cst_" + name), self.cc_all)
    ib, ob, ibb, obb, st, cc = self.coll[name]
    o = 0
    for (ap, bufs, n) in parts:
        p.dma(p.sp, ib.ap()[:, o:o + n], ap, st, r=bufs, w=[ibb])
        o += n
    ibb.w = {id(st.sem): (st.sem, st.cnt)}
    p._waits(p.pool, [ibb], [obb])
    ins = p.pool.h.collective_compute("AllGather", ALU.bypass, replica_groups=[list(range(NCORES))],
                                      ins=[ib.ap().opt()], outs=[ob.ap().opt()])
    cc.cnt += 1
    ins.then_inc(cc.sem, 1)
    tok = (cc.sem, cc.cnt)
    ibb.r[id(cc.sem)] = tok
    obb.w = {id(cc.sem): tok}
    obb.r = {}
    return ob.ap(), obb


def _select_prev(self, ob, obb, ncol, dt, acc, accb, tmp, tmpb, ld, selbase=0, tmp_dma=None, ncol_dma=None):
    p = self.p
    p.op(p.dve, lambda e: e.memset(acc, 0.0), w=[accb])
    for r in range(NCORES):
        i = r % 2
        if tmp_dma is None:
            p.dma(p.sp, tmp[i], ob[r * 128:(r + 1) * 128, :], ld[i], r=[obb], w=[tmpb[i]])
        else:
            p.dma(p.sp, tmp_dma[i], ob[r * 128:(r + 1) * 128, 0:ncol_dma], ld[i], r=[obb], w=[tmpb[i]])
        p.op(p.dve, lambda e: e.scalar_tensor_tensor(out=acc, in0=tmp[i], scalar=self.sel[:, selbase + r:selbase + r + 1], in1=acc, op0=ALU.mult, op1=ALU.add),
             r=[tmpb[i], self.sel_b, accb], w=[accb])


def _exchange_kv(self, KT, KTb, VX, VXb, KOFF):
    p = self.p
    A = self.arena
    bm = Bump(self); bm.off = self.xk_off
    acc = bm.take(BF16, 1024); accb = A.buf("xacc")
    tmp = [bm.take(BF16, 1024) for _ in range(2)]; tmpb = A.bufs("xtmp", 2)
    parts = [(KT[:, :, KOFF + PT - 128:KOFF + PT], [KTb[2]], 512), (VX[:, 8, :, :], [VXb[8]], 512)]
    p2 = []
    ob, obb = self.coll_gather("kv", [(ap, b, n) for (ap, b, n) in parts], 1024, BF16) if False else (None, None)
    ib_parts = []
    for (ap, b, n) in parts:
        ib_parts.append((ap, b, n))
    ob, obb = self.coll_gather3("kv", ib_parts, 1024, BF16)
    self.select_prev(ob, obb, 1024, BF16, acc, accb, tmp, tmpb, self.ld_x2)
    p.op(p.dve, lambda e: e.tensor_copy(out=KT[:, :, 0:KOFF], in_=acc[:, 0:512].rearrange("p (k s) -> p k s", s=128)), r=[accb], w=[KTb[0]])
    p.op(p.dve, lambda e: e.tensor_copy(out=VX[:, 0, :, :], in_=acc[:, 512:1024].rearrange("p (k s) -> p k s", s=128)), r=[accb], w=[VXb[0]])


def _coll_gather3(self, name, parts, ncol, dt):
    p = self.p
    if name not in self.coll:
        ib = self.nc.dram_tensor("ib_" + name, [128, ncol], dt)
        ob = self.nc.dram_tensor("ob_" + name, [NCORES * 128, ncol], dt)
        if not hasattr(self, "cc_all"):
            self.cc_all = DSem(p.sem("cc_all"))
        self.coll[name] = (ib, ob, Buf("ib_" + name), Buf("ob_" + name), p.dsem("cst_" + name), self.cc_all)
    ib, ob, ibb, obb, st, cc = self.coll[name]
    o = 0
    for (ap, bufs, n) in parts:
        dst = ib.ap()[:, o:o + n]
        if len(ap.shape) == 3:
            dst = dst.rearrange("p (a b) -> p a b", b=ap.shape[2])
        p.dma(p.sp, dst, ap, st, r=bufs, w=[ibb])
        o += n
    ibb.w = {id(st.sem): (st.sem, st.cnt)}
    p._waits(p.pool, [ibb], [obb])
    ins = p.pool.h.collective_compute("AllGather", ALU.bypass, replica_groups=[list(range(NCORES))],
                                      ins=[ib.ap().opt()], outs=[ob.ap().opt()])
    cc.cnt += 1
    ins.then_inc(cc.sem, 1)
    tok = (cc.sem, cc.cnt)
    ibb.r[id(cc.sem)] = tok
    obb.w = {id(cc.sem): tok}
    obb.r = {}
    return ob.ap(), obb


Kernel.coll_gather = _coll_gather
Kernel.coll_gather3 = _coll_gather3
Kernel.select_prev = _select_prev
Kernel.exchange_kv = _exchange_kv


TB = 16
NBK = PT // TB
NSC = NBK + 4


def _plan_ssm(self, l, li):
    js = self.ssm_idx[l]
    self.plan_cols(lambda jb: ("win", l, jb), self.d_wsin[js * D:(js + 1) * D, :], KC, 8, 256)
    Wg = self.d_wglu[js * D:(js + 1) * D, :]
    for ti in range(3):
        for m in range(KC):
            def fn(slot, issue, m=m):
                sv = slot[:, 0:4096].rearrange("p (k m) -> p k m", m=256)
                issue(sv[:, :, 0:128], Wg[:, m * 128:(m + 1) * 128].rearrange("(k p) m -> p k m", p=128))
                issue(sv[:, :, 128:256], Wg[:, D + m * 128:D + (m + 1) * 128].rearrange("(k p) m -> p k m", p=128))
            self.ws_jobs.append((("glu", l, ti, m), fn))


def _ssm(self, l, li):
    p = self.p
    A = self.arena
    js = self.ssm_idx[l]
    A.reset()
    bm = Bump(self)
    U = bm.take(BF16, KC, NT)
    Ub = [[A.buf(f"u{c}_{t}") for t in range(3)] for c in range(KC)]
    STo = bm.off
    bm.off += 2 * 64 * NSC * 4
    FR = [self.wv(STo, F32, 64, NSC), self.wv(STo + 64 * NSC * 4, F32, 64, NSC)]
    FRb = A.buf("FR")
    sm = {}
    smb = A.buf("ssm_small")
    for nm in ("lr", "li", "dt", "mag", "ang", "s16", "c8", "s8", "ar", "ai", "cr", "ci", "den", "a16r", "a16i", "akr", "aki",
               "t1", "t2", "t3", "t4", "sr", "si", "hir", "hii", "one"):
        sm[nm] = bm.take(F32, 64)
    PW = [bm.take(F32, 64, 17), bm.take(F32, 64, 17)]
    hs_in = [bm.take(F32, 64, 4), bm.take(F32, 64, 4)]
    hs_out = [bm.take(F32, 64, 4), bm.take(F32, 64, 4)]
    msk = bm.take(F32, 8)
    dsk = bm.take(F32, KC)
    BC = [bm.take(F32, 4, 16) for _ in range(4)]; BCb = A.buf("BC")
    bbar = [bm.take(F32, 4, 16), bm.take(F32, 4, 16)]
    T1 = bm.take(F32, 9 * 64); T2 = bm.take(F32, 9 * 64)
    HP = [bm.take(BF16, 4, NSC + 1), bm.take(BF16, 4, NSC + 1)]; HPb = A.buf("HP")
    g1 = bm.take(F32, 512); g2_ = bm.take(F32, 512); gb = A.bufs("g", 2)
    sq = [bm.take(BF16, 512) for _ in range(2)]; sqb = A.bufs("sq", 2)
    rstd = bm.take(F32, 512); rstdb = A.buf("rstd")
    xo = bm.off
    persist = list(A.live)
    EB = bm.take(BF16, 16, 128); EBb = A.buf("EB")
    BAT = bm.take(BF16, 2, 16, 128); BATb = A.buf("BAT")
    TT = lambda e, o, a, b, op: e.tensor_tensor(out=o, in0=a, in1=b, op=op)
    R = [smb]

    def dv(fn, r=(), w=()):
        p.op(p.dve, fn, r=R + list(r), w=R + list(w))

    def ac(fn, r=(), w=()):
        p.op(p.act, fn, r=R + list(r), w=R + list(w))

    def cmul(orr, oi, xr, xi, yr, yi, t1, t2):
        dv(lambda e: TT(e, t1, xr, yr, ALU.mult)); dv(lambda e: TT(e, t2, xi, yi, ALU.mult)); dv(lambda e: TT(e, orr, t1, t2, ALU.subtract))
        dv(lambda e: TT(e, t1, xr, yi, ALU.mult)); dv(lambda e: TT(e, t2, xi, yr, ALU.mult)); dv(lambda e: TT(e, oi, t1, t2, ALU.add))
    prm = self.d_sprm[js]
    for i, nm in enumerate(("lr", "li", "dt")):
        p.dma(p.sp, sm[nm], prm[:, i * 64:(i + 1) * 64], self.ld_sp[i], w=[smb])
    p.dma(p.sp, msk, self.d_smask, self.ld_sp[3], w=[smb])
    p.dma(p.sp, dsk, self.d_sd[js], self.ld_sp[4], w=[smb])
    p.dma(p.sp, hs_in[0].rearrange("p a b -> p (a b)"), self.d_sst[js][0], self.ld_sp[5], w=[smb])
    p.dma(p.sp, hs_in[1].rearrange("p a b -> p (a b)"), self.d_sst[js][1], self.ld_sp[6], w=[smb])
    smb.w = {id(s_.sem): (s_.sem, s_.cnt) for s_ in self.ld_sp}
    s = sm
    ac(lambda e: e.activation(out=s["dt"], in_=s["dt"], func=AF.Exp))
    dv(lambda e: TT(e, s["t1"], s["lr"], s["dt"], ALU.mult))
    ac(lambda e: e.activation(out=s["mag"], in_=s["t1"], func=AF.Exp))
    dv(lambda e: TT(e, s["ang"], s["li"], s["dt"], ALU.mult))
    ac(lambda e: e.activation(out=s["s16"], in_=s["ang"], func=AF.Sin, scale=1.0 / 16))
    ac(lambda e: e.activation(out=s["s8"], in_=s["ang"], func=AF.Sin, scale=1.0 / 8))
    dv(lambda e: TT(e, s["t1"], s["s16"], s["s16"], ALU.mult))
    dv(lambda e: e.tensor_scalar(out=s["c8"], in0=s["t1"], scalar1=-2.0, scalar2=1.0, op0=ALU.mult, op1=ALU.add))
    cr_, ci_ = s["c8"], s["s8"]
    for it in range(3):
        dv(lambda e: TT(e, s["t1"], cr_, cr_, ALU.mult)); dv(lambda e: TT(e, s["t2"], ci_, ci_, ALU.mult))
        dv(lambda e: TT(e, s["t3"], cr_, ci_, ALU.mult))
        dv(lambda e: TT(e, cr_, s["t1"], s["t2"], ALU.subtract))
        dv(lambda e: e.tensor_scalar(out=ci_, in0=s["t3"], scalar1=2.0, scalar2=None, op0=ALU.mult))
    dv(lambda e: TT(e, s["ar"], s["mag"], cr_, ALU.mult)); dv(lambda e: TT(e, s["ai"], s["mag"], ci_, ALU.mult))
    dv(lambda e: e.tensor_scalar(out=s["t4"], in0=s["ar"], scalar1=-1.0, scalar2=None, op0=ALU.add))
    dv(lambda e: TT(e, s["t1"], s["t4"], s["lr"], ALU.mult)); dv(lambda e: TT(e, s["t2"], s["ai"], s["li"], ALU.mult))
    dv(lambda e: TT(e, s["cr"], s["t1"], s["t2"], ALU.add))
    dv(lambda e: TT(e, s["t1"], s["ai"], s["lr"], ALU.mult)); dv(lambda e: TT(e, s["t2"], s["t4"], s["li"], ALU.mult))
    dv(lambda e: TT(e, s["ci"], s["t1"], s["t2"], ALU.subtract))
    dv(lambda e: TT(e, s["t1"], s["lr"], s["lr"], ALU.mult)); dv(lambda e: TT(e, s["t2"], s["li"], s["li"], ALU.mult))
    dv(lambda e: TT(e, s["den"], s["t1"], s["t2"], ALU.add)); dv(lambda e: e.reciprocal(out=s["den"], in_=s["den"]))
    dv(lambda e: TT(e, s["cr"], s["cr"], s["den"], ALU.mult)); dv(lambda e: TT(e, s["ci"], s["ci"], s["den"], ALU.mult))
    dv(lambda e: e.memset(PW[0][:, :, 0:1], 1.0)); dv(lambda e: e.memset(PW[1][:, :, 0:1], 0.0))
    dv(lambda e: e.tensor_copy(out=PW[0][:, :, 1], in_=s["ar"])); dv(lambda e: e.tensor_copy(out=PW[1][:, :, 1], in_=s["ai"]))
    t1v = lambda n: T1[:, 0:64 * n].rearrange("p (a b) -> p a b", b=n)
    t2v = lambda n: T2[:, 0:64 * n].rearrange("p (a b) -> p a b", b=n)
    k = 1
    while k < 16:
        br = PW[0][:, :, k:k + 1].to_broadcast([128, 64, k]); bi = PW[1][:, :, k:k + 1].to_broadcast([128, 64, k])
        cmul(PW[0][:, :, k + 1:2 * k + 1], PW[1][:, :, k + 1:2 * k + 1], PW[0][:, :, 1:k + 1], PW[1][:, :, 1:k + 1], br, bi, t1v(k), t2v(k))
        k *= 2
    dv(lambda e: e.tensor_copy(out=s["a16r"], in_=PW[0][:, :, 16])); dv(lambda e: e.tensor_copy(out=s["a16i"], in_=PW[1][:, :, 16]))
    dv(lambda e: e.tensor_copy(out=s["akr"], in_=s["a16r"])); dv(lambda e: e.tensor_copy(out=s["aki"], in_=s["a16i"]))
    for it in range(6):
        cmul(s["t3"], s["t4"], s["akr"], s["aki"], s["akr"], s["aki"], s["t1"], s["t2"])
        dv(lambda e: e.tensor_copy(out=s["akr"], in_=s["t3"])); dv(lambda e: e.tensor_copy(out=s["aki"], in_=s["t4"]))
    Hb = self.wv(STo, BF16, KC, NT)
    hbb = [A.buf(f"hb{i}") for i in range(3)]
    items = [(t, c0, n, (lambda c, c0=c0, n=n: Hb[:, c, c0:c0 + n]), hbb[i]) for i, (t, c0, n) in enumerate(X_TILES)]
    self.norm_pre(l, 0, items, sq, sqb, rstd, rstdb)

    def usink(m, ti, t, c0, n, ps, psb):
        bl = TB if t != 2 else 8
        p.op(p.act, lambda e: e.activation(out=U[:, m, c0:c0 + n].rearrange("p (i k) -> p k i", i=bl),
                                           in_=ps[:, 0:n].rearrange("p (k i) -> p k i", i=bl), func=AF.Copy), r=[psb], w=[Ub[m][ti]])
    self.proj_fm(lambda jb: ("win", l, jb), 8, 2, KC, 256, lambda k_, c0, n: Hb[:, k_, c0:c0 + n], lambda ti: [hbb[ti]], X_TILES, usink)
    FRb.r = dict(_hz(hbb))
    mg2 = lambda g: msk[:, g:g + 1]
    mpl = lambda q: msk[:, 2 + q:3 + q]

    def load_bc(c):
        for i in range(4):
            p.dma(p.sp, BC[i].rearrange("p a b -> p (a b)"), self.d_sbc[js][i][:, c * 64:(c + 1) * 64], self.ld_bc[i], w=[BCb])
        BCb.w = {id(s_.sem): (s_.sem, s_.cnt) for s_ in self.ld_bc}
        cr4 = s["cr"][:, 4 * c:4 * c + 4].unsqueeze(2).to_broadcast([128, 4, 16])
        ci4 = s["ci"][:, 4 * c:4 * c + 4].unsqueeze(2).to_broadcast([128, 4, 16])
        t1 = T1[:, 0:64].rearrange("p (a b) -> p a b", b=16); t2 = T2[:, 0:64].rearrange("p (a b) -> p a b", b=16)
        p.op(p.dve, lambda e: TT(e, t1, cr4, BC[0], ALU.mult), r=R + [BCb], w=R)
        p.op(p.dve, lambda e: TT(e, t2, ci4, BC[1], ALU.mult), r=R + [BCb], w=R)
        dv(lambda e: TT(e, bbar[0], t1, t2, ALU.subtract))
        p.op(p.dve, lambda e: TT(e, t1, cr4, BC[1], ALU.mult), r=R + [BCb], w=R)
        p.op(p.dve, lambda e: TT(e, t2, ci4, BC[0], ALU.mult), r=R + [BCb], w=R)
        dv(lambda e: TT(e, bbar[1], t1, t2, ALU.add))

    def pw4(ri, c, m0, nm):
        return PW[ri][:, 4 * c:4 * c + 4, m0:m0 + nm].rearrange("p a m -> p m a").unsqueeze(3).to_broadcast([128, nm, 4, 16])

    def bc4(ap, nm):
        return ap.unsqueeze(1).to_broadcast([128, nm, 4, 16])

    def t4v(T, nm):
        return T[:, 0:nm * 64].rearrange("p (m a b) -> p m a b", a=4, b=16)

    def expand(dst, src, nm):
        for g in range(2):
            p.op(p.dve, lambda e: e.tensor_scalar(out=dst[:, :, :, g, :], in0=src, scalar1=mg2(g), scalar2=None, op0=ALU.mult), r=R, w=R + [dstb_[0]])
    dstb_ = [EBb]
    for c in range(KC):
        load_bc(c)
        for ri in range(2):
            for mh in range(2):
                m0 = 8 * mh
                a_, b_ = t4v(T1, 8), t4v(T2, 8)
                if ri == 0:
                    dv(lambda e: TT(e, a_, pw4(0, c, m0, 8), bc4(bbar[0], 8), ALU.mult)); dv(lambda e: TT(e, b_, pw4(1, c, m0, 8), bc4(bbar[1], 8), ALU.mult))
                    dv(lambda e: TT(e, a_, a_, b_, ALU.subtract))
                else:
                    dv(lambda e: TT(e, a_, pw4(0, c, m0, 8), bc4(bbar[1], 8), ALU.mult)); dv(lambda e: TT(e, b_, pw4(1, c, m0, 8), bc4(bbar[0], 8), ALU.mult))
                    dv(lambda e: TT(e, a_, a_, b_, ALU.add))
                dstb_[0] = EBb
                expand(EB[:, m0:m0 + 8, :].rearrange("p m (a g h) -> p m a g h", a=4, g=2), a_, 8)
            for q4 in range(2):
                T, Tb = self.psum()
                Tv = T[:].bitcast(BF16)
                for ii in range(8):
                    i = q4 * 8 + ii
                    p.op(p.pe, lambda e: e.transpose(out=Tv[:, ii * 128:(ii + 1) * 128], in_=EB[:, 15 - i, :], identity=self.ident),
                         r=[EBb, self.cb16_b], w=[Tb], sig=(ii == 7))
                p.op(p.act, lambda e: e.activation(out=BAT[:, ri, q4 * 8:(q4 + 1) * 8, :].rearrange("p a b -> p (a b)"), in_=Tv[:, 0:1024], func=AF.Copy),
                     r=[Tb], w=[BATb])
        for pl in range(4):
            pr = 4 * c + pl
            for ri in range(2):
                F, Fb = self.psum()
                for ta in range(2):
                    for i in range(TB):
                        p.op(p.pe, lambda e: e.matmul(F[:, 32 * ta:32 * ta + 32], lhsT=BAT[32 * pl:32 * pl + 32, ri, i, :],
                                                      rhs=U[32 * pl:32 * pl + 32, c, 512 * ta + 32 * i:512 * ta + 32 * i + 32],
                                                      start=(i == 0), stop=(i == TB - 1), tile_position=(32 * pl, 0)),
                             r=[BATb, Ub[c][ta]], w=[Fb], sig=(i == TB - 1))
                for i in range(8):
                    p.op(p.pe, lambda e: e.matmul(F[:, NBK:NSC], lhsT=BAT[32 * pl:32 * pl + 32, ri, 8 + i, :],
                                                  rhs=U[32 * pl:32 * pl + 32, c, PT + 4 * i:PT + 4 * i + 4],
                                                  start=(i == 0), stop=(i == 7), tile_position=(32 * pl, 0)),
                         r=[BATb, Ub[c][2]], w=[Fb], sig=(i == 7))
                p.op(p.act, lambda e: e.activation(out=FR[ri][:, pr, :], in_=F[:, 0:NSC], func=AF.Copy), r=[Fb], w=[FRb])
    def scan(store):
        for kk in range(NBK):
            dv(lambda e: TT(e, s["t1"], s["a16r"], s["sr"], ALU.mult), r=[FRb]); dv(lambda e: TT(e, s["t2"], s["a16i"], s["si"], ALU.mult))
            dv(lambda e: TT(e, s["t3"], s["a16r"], s["si"], ALU.mult)); dv(lambda e: TT(e, s["t4"], s["a16i"], s["sr"], ALU.mult))
            dv(lambda e: TT(e, s["t1"], s["t1"], s["t2"], ALU.subtract)); dv(lambda e: TT(e, s["t3"], s["t3"], s["t4"], ALU.add))
            if store:
                p.op(p.dve, lambda e: TT(e, FR[0][:, :, kk], s["t1"], FR[0][:, :, kk], ALU.add), r=R + [FRb], w=R + [FRb])
                p.op(p.dve, lambda e: TT(e, FR[1][:, :, kk], s["t3"], FR[1][:, :, kk], ALU.add), r=R + [FRb], w=R + [FRb])
                dv(lambda e: e.tensor_copy(out=s["sr"], in_=FR[0][:, :, kk]), r=[FRb]); dv(lambda e: e.tensor_copy(out=s["si"], in_=FR[1][:, :, kk]), r=[FRb])
            else:
                dv(lambda e: TT(e, s["sr"], s["t1"], FR[0][:, :, kk], ALU.add), r=[FRb]); dv(lambda e: TT(e, s["si"], s["t3"], FR[1][:, :, kk], ALU.add), r=[FRb])
    dv(lambda e: e.memset(s["hir"], 0.0)); dv(lambda e: e.memset(s["hii"], 0.0))
    if self.cfg.get("exchange"):
        dv(lambda e: e.memset(s["sr"], 0.0)); dv(lambda e: e.memset(s["si"], 0.0))
        scan(False)
        if not self.cfg.get("no_ssm_coll"):
            self.exchange_ssm(s, smb, dv, cmul, xo)
    dv(lambda e: e.tensor_copy(out=s["sr"], in_=s["hir"])); dv(lambda e: e.tensor_copy(out=s["si"], in_=s["hii"]))
    scan(True)
    a8r = PW[0][:, :, 8:9].to_broadcast([128, 64, 4]); a8i = PW[1][:, :, 8:9].to_broadcast([128, 64, 4])
    t1s, t2s = t1v(4), t2v(4)
    cmul(hs_out[0], hs_out[1], hs_in[0], hs_in[1], a8r, a8i, t1s, t2s)
    p.op(p.dve, lambda e: TT(e, hs_out[0], hs_out[0], FR[0][:, :, NBK:NSC], ALU.add), r=R + [FRb], w=R)
    p.op(p.dve, lambda e: TT(e, hs_out[1], hs_out[1], FR[1][:, :, NBK:NSC], ALU.add), r=R + [FRb], w=R)
    p.dma(p.sp, self.o_ssp[js][0], FR[0][:, :, NBK - 1], self.st_ss[0], r=[FRb], allow_slow_non_contiguous=True)
    p.dma(p.sp, self.o_ssp[js][1], FR[1][:, :, NBK - 1], self.st_ss[1], r=[FRb], allow_slow_non_contiguous=True)
    p.dma(p.sp, self.o_sss[js][0], hs_out[0].rearrange("p a b -> p (a b)"), self.st_ss[2], r=[smb])
    p.dma(p.sp, self.o_sss[js][1], hs_out[1].rearrange("p a b -> p (a b)"), self.st_ss[3], r=[smb])
    A.reset(); A.live.extend(persist)
    sub = Bump(self); sub.off = xo
    ECA = sub.take(BF16, 2, 17, 128); ECAb = A.buf("ECA")
    Wt = sub.take(BF16, TB, 128); Wtb = A.buf("Wt")
    EB0 = sub.take(BF16, 2, 128); EB0b = A.buf("EB0")
    gc = math.sqrt(2.0 / math.pi)
    for c in range(KC):
        load_bc(c)
        for ri in range(2):
            for g in range(2):
                p.op(p.dve, lambda e: e.tensor_scalar(out=EB0[:, ri, :].rearrange("p (a g h) -> p a g h", a=4, g=2)[:, :, g, :], in0=bbar[ri],
                                                      scalar1=mg2(g), scalar2=None, op0=ALU.mult), r=R, w=R + [EB0b])
        for ri in range(2):
            for (m0, nm) in ((0, 9), (9, 8)):
                a_, b_ = t4v(T1, nm), t4v(T2, nm)
                if ri == 0:
                    p.op(p.dve, lambda e: TT(e, a_, pw4(0, c, m0, nm), bc4(BC[2], nm), ALU.mult), r=R + [BCb], w=R)
                    p.op(p.dve, lambda e: TT(e, b_, pw4(1, c, m0, nm), bc4(BC[3], nm), ALU.mult), r=R + [BCb], w=R)
                    dv(lambda e: TT(e, a_, a_, b_, ALU.subtract))
                else:
                    p.op(p.dve, lambda e: TT(e, a_, pw4(1, c, m0, nm), bc4(BC[2], nm), ALU.mult), r=R + [BCb], w=R)
                    p.op(p.dve, lambda e: TT(e, b_, pw4(0, c, m0, nm), bc4(BC[3], nm), ALU.mult), r=R + [BCb], w=R)
                    dv(lambda e: TT(e, a_, a_, b_, ALU.add))
                    dv(lambda e: e.tensor_scalar(out=a_, in0=a_, scalar1=-1.0, scalar2=None, op0=ALU.mult))
                dstb_[0] = ECAb
                expand(ECA[:, ri, m0:m0 + nm, :].rearrange("p m (a g h) -> p m a g h", a=4, g=2), a_, nm)
        TP, TPb = self.psum()
        for pl in range(4):
            for ri in range(2):
                p.op(p.pe, lambda e: e.matmul(TP[32 * pl:32 * pl + 32, 0:512],
                                              lhsT=EB0[:, ri, 32 * pl:32 * pl + 32],
                                              rhs=ECA[:, ri, 0:16, 32 * pl:32 * pl + 32], start=(ri == 0), stop=(ri == 1),
                                              tile_position=(0, 32 * pl)), r=[EB0b, ECAb], w=[TPb], sig=(ri == 1))
        TPv = TP[:, 0:512].rearrange("p (m x) -> p m x", x=32)
        for q in range(4):
            p.op(p.dve, lambda e: e.tensor_scalar(out=Wt[:, :, 32 * q:32 * q + 32], in0=TPv, scalar1=mpl(q), scalar2=None, op0=ALU.mult),
                 r=R + [TPb], w=[Wtb])
        p.op(p.dve, lambda e: e.scalar_tensor_tensor(out=Wt[:, 0, :], in0=self.ident, scalar=dsk[:, c:c + 1], in1=Wt[:, 0, :],
                                                     op0=ALU.mult, op1=ALU.add), r=R + [self.cb16_b, Wtb], w=[Wtb])
        for ri in range(2):
            hin = s["hir"] if ri == 0 else s["hii"]
            p.op(p.dve, lambda e: e.tensor_copy(out=HP[ri][:, :, 0], in_=hin[:, 4 * c:4 * c + 4]), r=R, w=[HPb])
            p.op(p.dve, lambda e: e.tensor_copy(out=HP[ri][:, :, 1:NBK], in_=FR[ri][:, 4 * c:4 * c + 4, 0:NBK - 1]), r=[FRb], w=[HPb])
            p.op(p.dve, lambda e: e.tensor_copy(out=HP[ri][:, :, NBK:NBK + 4], in_=hs_in[ri][:, 4 * c:4 * c + 4, :]), r=R, w=[HPb])
        for ti, (t, c0, n) in enumerate(X_TILES):
            Y, Yb = self.psum()
            if t != 2:
                nb, bl, k0 = n // TB, TB, c0 // TB
            else:
                nb, bl, k0 = 4, 8, NBK
            for tau in range(bl):
                p.op(p.pe, lambda e: e.matmul(Y[:, tau * nb:n], lhsT=Wt[:, tau, :], rhs=U[:, c, c0:c0 + (bl - tau) * nb], start=(tau == 0), stop=False, skip_group_check=True),
                     r=[Wtb, Ub[c][ti]], w=[Yb], sig=False)
            for jj in range(bl):
                for pl in range(4):
                    for ri in range(2):
                        last = (jj == bl - 1 and pl == 3 and ri == 1)
                        p.op(p.pe, lambda e: e.matmul(Y[32 * pl:32 * pl + 32, jj * nb:(jj + 1) * nb],
                                                      lhsT=ECA[:, ri, jj + 1, 32 * pl:32 * pl + 32], rhs=HP[ri][:, pl, k0:k0 + nb],
                                                      start=False, stop=(ri == 1), tile_position=(0, 32 * pl), skip_group_check=True),
                             r=[ECAb, HPb], w=[Yb], sig=last)
            p.op(p.act, lambda e: e.activation(out=g1[:, 0:n], in_=Y[:, 0:n], func=AF.Square), r=[Yb], w=[gb[0]])
            p.op(p.dve, lambda e: e.tensor_scalar(out=g1[:, 0:n], in0=g1[:, 0:n], scalar1=0.044715, scalar2=1.0, op0=ALU.mult, op1=ALU.add), r=[gb[0]], w=[gb[0]])
            p.op(p.dve, lambda e: TT(e, g1[:, 0:n], g1[:, 0:n], Y[:, 0:n], ALU.mult), r=[gb[0], Yb], w=[gb[0]])
            p.op(p.act, lambda e: e.activation(out=g2_[:, 0:n], in_=g1[:, 0:n], func=AF.Sigmoid, scale=2.0 * gc), r=[gb[0]], w=[gb[1]])
            p.op(p.dve, lambda e: TT(e, U[:, c, c0:c0 + n].rearrange("p (k i) -> p i k", i=bl), g2_[:, 0:n].rearrange("p (i k) -> p i k", i=bl),
                                     Y[:, 0:n].rearrange("p (i k) -> p i k", i=bl), ALU.mult), r=[gb[1], Yb], w=[Ub[c][ti]])
    A.reset(); A.live.extend(persist)
    f = self.wv(STo, F32, KC, 512); fb = A.bufs("f", KC)
    tmp = [g1, g2_]; tmpb = gb
    for ti, (t, c0, n) in enumerate(X_TILES):
        for m in range(KC):
            slot, sb_ = self.ws_get(("glu", l, ti, m))
            sv = slot[:, 0:4096].rearrange("p (k m) -> p k m", m=256)
            pv, pvb = self.psum()
            pg, pgb = self.psum()
            for (ps, psb, m0) in ((pv, pvb, 0), (pg, pgb, 128)):
                for k_ in range(KC):
                    p.op(p.pe, lambda e: e.matmul(ps[:, 0:n], lhsT=sv[:, k_, m0:m0 + 128], rhs=U[:, k_, c0:c0 + n], start=(k_ == 0), stop=(k_ == KC - 1)),
                         r=[sb_, Ub[k_][ti]], w=[psb], sig=(k_ == KC - 1))
            i = m % 2
            p.op(p.act, lambda e: e.activation(out=tmp[i][:, 0:n], in_=pg[:, 0:n], func=AF.Sigmoid), r=[pgb], w=[tmpb[i]])
            p.op(p.dve, lambda e: TT(e, f[:, m, 0:n], pv[:, 0:n], tmp[i][:, 0:n], ALU.mult), r=[pvb, tmpb[i]], w=[fb[m]])
            self.ws_prefetch()
        self.post_norm(l, 1, [(t, c0, n, (lambda c, n=n: f[:, c, 0:n]), (lambda c: [fb[c]]))], sq, sqb, rstd, rstdb, tmp, tmpb)


def _hz(bufs):
    d = {}
    for b in bufs:
        _merge(d, b.hazards())
    return d


Kernel.plan_ssm = _plan_ssm
Kernel.ssm = _ssm


def _exchange_ssm(self, s, smb, dv, cmul, xo):
    p = self.p
    A = self.arena
    TT = lambda e, o, a, b, op: e.tensor_tensor(out=o, in0=a, in1=b, op=op)
    bm = Bump(self); bm.off = xo
    pay = bm.take(F32, 128); payb = A.buf("spay")
    tmp = [bm.take(F32, 128) for _ in range(2)]; tmpb = A.bufs("stmp", 2)
    acc = [bm.take(F32, 128) for _ in range(3)]; accb = A.bufs("sacc", 3)
    p.op(p.dve, lambda e: e.tensor_copy(out=pay[:, 0:64], in_=s["sr"]), r=[smb], w=[payb])
    p.op(p.dve, lambda e: e.tensor_copy(out=pay[:, 64:128], in_=s["si"]), r=[smb], w=[payb])
    ob, obb = self.coll_gather3("kv", [(pay.bitcast(BF16), [payb], 256)], 1024, BF16)
    for m in range(3):
        self.select_prev(ob, obb, 128, F32, acc[m], accb[m], tmp, tmpb, self.ld_x2, selbase=8 * m,
                         tmp_dma=[t_.bitcast(BF16) for t_ in tmp], ncol_dma=256)
    cmul(s["mag"], s["ang"], s["akr"], s["aki"], s["akr"], s["aki"], s["t1"], s["t2"])
    p.op(p.dve, lambda e: e.tensor_copy(out=s["hir"], in_=acc[0][:, 0:64]), r=[smb, accb[0]], w=[smb])
    p.op(p.dve, lambda e: e.tensor_copy(out=s["hii"], in_=acc[0][:, 64:128]), r=[smb, accb[0]], w=[smb])
    for m, (pr_, pi_) in ((1, (s["akr"], s["aki"])), (2, (s["mag"], s["ang"]))):
        p.op(p.dve, lambda e: e.tensor_copy(out=s["c8"], in_=acc[m][:, 0:64]), r=[smb, accb[m]], w=[smb])
        p.op(p.dve, lambda e: e.tensor_copy(out=s["s8"], in_=acc[m][:, 64:128]), r=[smb, accb[m]], w=[smb])
        cmul(s["t3"], s["t4"], pr_, pi_, s["c8"], s["s8"], s["t1"], s["t2"])
        dv(lambda e: TT(e, s["hir"], s["hir"], s["t3"], ALU.add))
        dv(lambda e: TT(e, s["hii"], s["hii"], s["t4"], ALU.add))


def _exchange_halo(self):
    p = self.p
    xl = [self.xb[c][1] for c in range(KC)]
    ob, obb = self.coll_gather3("halo", [(self.X[:, :, PT - 2:PT], xl, 32)], 32, F32)
    self.select_prev(ob, obb, 32, F32, self.xh_acc[:], self.xh_accb, [self.xh_tmp[:, 0, :], self.xh_tmp[:, 1, :]], self.xh_tmpb, self.ld_x2)
    hz = [self.xb[c][3] for c in range(KC)]
    p.op(p.dve, lambda e: e.tensor_copy(out=self.X[:, :, NT:NTH], in_=self.xh_acc[:].rearrange("p (c j) -> p c j", j=2)), r=[self.xh_accb], w=hz)


Kernel.exchange_ssm = _exchange_ssm
Kernel.exchange_halo = _exchange_halo


FULL_CFG = {"layers": [0, 1, 2, 3], "stages": ["attn", "ssm", "xattn", "ffn"], "exchange": True}
_CACHE = {}


def kernel(**inputs):
    inp = {k: np.asarray(v) for k, v in inputs.items()}
    cfg = FULL_CFG
    if "nc" not in _CACHE:
        kern = Kernel(cfg)
        _CACHE["nc"] = kern.build()
    nc = _CACHE["nc"]
    maps = host_inputs(inp, cfg)
    res = run_bass_kernel_spmd(nc, maps, core_ids=list(range(NCORES)))
    R = res.results
    f32 = np.float32
    y_p = np.zeros((2, 4096, D), f32); y_s = np.zeros((32, 8, D), f32)
    kp = np.zeros((2, 2, 128, 4, 64), f32); vp = np.zeros_like(kp)
    ks = np.zeros((2, 32, 128, 4, 64), f32); vs = np.zeros_like(ks)
    srp = np.zeros((2, 2, 128, 64), f32); sip = np.zeros_like(srp)
    srs = np.zeros((2, 32, 128, 64), f32); sis = np.zeros_like(srs)
    cvp = np.zeros((4, 2, 2, 2 * DFF), f32); cvs = np.zeros((4, 32, 2, 2 * DFF), f32)
    mkp = np.zeros((4, 2, 256, 4, 128), f32); mvp = np.zeros_like(mkp)
    for c in range(NCORES):
        b, q = c // 4, c % 4
        r = R[c]
        yT = np.asarray(r["yT"])
        y_p[b, q * PT:(q + 1) * PT] = yT[:, 0:PT].T
        y_s[4 * c:4 * c + 4] = yT[:, PT:NT].T.reshape(4, 8, D)
        ks[:, 4 * c:4 * c + 4] = np.asarray(r["o_ks"]).reshape(2, 4, 128, 4, 64)
        vs[:, 4 * c:4 * c + 4] = np.asarray(r["o_vs"]).reshape(2, 4, 128, 4, 64)
        sss = np.asarray(r["o_sss"]).reshape(2, 2, 2, 64, 64, 4)
        sss = sss.transpose(0, 1, 5, 4, 2, 3).reshape(2, 2, 4, 128, 64)
        srs[:, 4 * c:4 * c + 4] = sss[:, 0]
        sis[:, 4 * c:4 * c + 4] = sss[:, 1]
        cvs[:, 4 * c:4 * c + 4] = np.asarray(r["o_conv_s"]).reshape(4, 128, 88, 4, 2).transpose(0, 3, 4, 2, 1).reshape(4, 4, 2, 2 * DFF)
        if q == 3:
            kp[:, b] = np.asarray(r["o_kp"]).reshape(2, 128, 4, 64)
            vp[:, b] = np.asarray(r["o_vp"]).reshape(2, 128, 4, 64)
            ssp = np.asarray(r["o_ssp"]).reshape(2, 2, 2, 64, 64)
            ssp = ssp.transpose(0, 1, 4, 2, 3).reshape(2, 2, 128, 64)
            srp[:, b] = ssp[:, 0]
            sip[:, b] = ssp[:, 1]
            cvp[:, b] = np.asarray(r["o_conv_p"]).reshape(4, 128, 88, 2).transpose(0, 3, 2, 1).reshape(4, 2, 2 * DFF)
        if q == 0:
            mkp[:, b] = np.asarray(r["o_memk"]).reshape(4, 256, 4, 128)
            mvp[:, b] = np.asarray(r["o_memv"]).reshape(4, 256, 4, 128)
    return (y_p, y_s, kp, vp, ks, vs, srp, sip, srs, sis, cvp, cvs, mkp, mvp)
```
